# Optimizing a Trainium2 kernel written in Bass

```python
import math
import jax, jax.numpy as jnp
from jax import lax
import numpy as np

D_MODEL = 1024
BATCH = 8
SEQ = 2048
DEPTH = 2
DEC_BATCH = 128
DEC_SEQ = 8
PAST_LEN = 16384
PAGE_SIZE = 128

N_EVEN = (DEPTH + 1) // 2
N_ODD = DEPTH // 2
D_A = D_MODEL // 2
CONV_W = 3
B_HEAD_DIM = 128
B_HEADS = D_MODEL // (2 * B_HEAD_DIM)
D_B = B_HEADS * B_HEAD_DIM
D_IN = 3 * D_A + 4 * D_B
CHUNK = 64
POOL_WINDOWS = (2, 4, 8, 16)
N_POOL_GROUPS = len(POOL_WINDOWS)
POOL_GROUP = D_MODEL // N_POOL_GROUPS
POOL_BUF = max(POOL_WINDOWS) - 1
D_FF = 2816
FFN_CONV_W = 3
EPS = 1e-6

kernel_name = "hybrid_shortconv_hgrn2_pool_convffn_step"


def rmsnorm(x, g):
    xf = x.astype(jnp.float32)
    y = xf * lax.rsqrt(jnp.mean(xf * xf, axis=-1, keepdims=True) + EPS)
    return (y * g.astype(jnp.float32)).astype(x.dtype)


def causal_dwconv(u, buf, w):
    K = w.shape[0]
    L = u.shape[1]
    ext = jnp.concatenate([buf.astype(u.dtype), u], axis=1)
    y = ext[:, 0:L] * w[0]
    for k in range(1, K):
        y = y + ext[:, k:k + L] * w[k]
    new_buf = ext[:, ext.shape[1] - (K - 1):]
    return y, new_buf


def hgrn2_recurrence(q, k, v, logf, s0):
    N, L, H, DK = q.shape
    c = math.gcd(L, CHUNK)
    nc = L // c

    def to_chunks(t):
        return t.astype(jnp.float32).reshape(N, nc, c, H, t.shape[-1]).transpose(1, 0, 3, 2, 4)

    qc, kc, vc, gc = to_chunks(q), to_chunks(k), to_chunks(v), to_chunks(logf)
    mask = jnp.tril(jnp.ones((c, c), dtype=bool))[:, :, None]

    def step(S, inp):
        qb, kb, vb, gb = inp
        G = jnp.cumsum(gb, axis=2)
        diff = G[:, :, :, None, :] - G[:, :, None, :, :]
        decay = jnp.exp(jnp.where(mask, diff, -jnp.inf))
        A = jnp.einsum('nhtd,nhsd,nhtsd->nhts', qb, kb, decay)
        o = (jnp.einsum('nhts,nhsv->nhtv', A, vb)
             + jnp.einsum('nhtd,nhdv->nhtv', qb * jnp.exp(G), S))
        G_last = G[:, :, -1:, :]
        S_new = (jnp.exp(G_last[:, :, 0, :])[..., None] * S
                 + jnp.einsum('nhsd,nhsv->nhdv', kb * jnp.exp(G_last - G), vb))
        return S_new, o

    S, o = lax.scan(step, s0.astype(jnp.float32), (qc, kc, vc, gc))
    o = o.transpose(1, 0, 3, 2, 4).reshape(N, L, H, v.shape[-1])
    return o, S


def even_mixer(h, conv_buf, s0, w_in, conv_w, lb, g_norm, w_out):
    N, L, _ = h.shape
    z = h @ w_in
    offs = [D_A, 2 * D_A, 3 * D_A, 3 * D_A + D_B, 3 * D_A + 2 * D_B, 3 * D_A + 3 * D_B]
    a_c, a_b, a_v, b_q, b_f, b_i, b_g = jnp.split(z, offs, axis=-1)
    conv_out, new_conv = causal_dwconv(a_c * a_v, conv_buf, conv_w)
    y_a = a_b * conv_out
    f = lb + (1.0 - lb) * jax.nn.sigmoid(b_f.astype(jnp.float32))
    logf = jnp.log(f)
    key = 1.0 - f
    hs = (N, L, B_HEADS, B_HEAD_DIM)
    o, S = hgrn2_recurrence(b_q.reshape(hs), key.reshape(hs), b_i.reshape(hs), logf.reshape(hs), s0)
    o = rmsnorm(o, g_norm.reshape(B_HEADS, B_HEAD_DIM)).reshape(N, L, D_B).astype(h.dtype)
    y_b = o * jax.nn.silu(b_g)
    y = jnp.concatenate([y_a, y_b], axis=-1) @ w_out
    return y, new_conv, S


def pool_mixer(h, buf, start, w_pool, scale):
    N, L, D = h.shape
    ext = jnp.concatenate([buf.astype(h.dtype), h], axis=1).astype(jnp.float32)
    cz = jnp.concatenate([jnp.zeros((N, 1, D), jnp.float32), jnp.cumsum(ext, axis=1)], axis=1)
    pos = start + jnp.arange(L)
    end = cz[:, POOL_BUF + 1:POOL_BUF + 1 + L]
    means = []
    for gi, w in enumerate(POOL_WINDOWS):
        sl = slice(gi * POOL_GROUP, (gi + 1) * POOL_GROUP)
        begin = cz[:, POOL_BUF + 1 - w:POOL_BUF + 1 - w + L, sl]
        cnt = jnp.minimum(w, pos + 1).astype(jnp.float32)[None, :, None]
        means.append((end[..., sl] - begin) / cnt)
    d = (jnp.concatenate(means, axis=-1) - ext[:, POOL_BUF:]).reshape(N, L, N_POOL_GROUPS, POOL_GROUP)
    y = jnp.einsum('nlgc,gce->nlge', d.astype(h.dtype), w_pool).reshape(N, L, D) * scale
    new_buf = ext[:, ext.shape[1] - POOL_BUF:].astype(h.dtype)
    return y, new_buf


def conv_ffn(h, buf, w_gu, conv_w, conv_b, w_down):
    gu = h @ w_gu
    g, u = jnp.split(gu, [D_FF], axis=-1)
    gc, new_buf = causal_dwconv(g, buf, conv_w)
    y = (jax.nn.silu(gc + conv_b) * u) @ w_down
    return y, new_buf


def trunk(x, start, s_conv, s_hgrn, s_pool, s_ffn, norm_mix, norm_ffn, norm_final, w_in, conv_a_w,
          hgrn_lower_bounds, hgrn_norm, w_out, pool_w, pool_scale, ffn_w_gu, ffn_conv_w, ffn_conv_b,
          ffn_w_down):
    lb_all = jnp.cumsum(jax.nn.softmax(hgrn_lower_bounds.astype(jnp.float32), axis=0), axis=0)
    new_conv, new_hgrn, new_pool, new_ffn = [], [], [], []
    for l in range(DEPTH):
        h = rmsnorm(x, norm_mix[l])
        if l % 2 == 0:
            e = l // 2
            y, cb, S = even_mixer(h, s_conv[e], s_hgrn[e], w_in[e], conv_a_w[e], lb_all[l],
                                  hgrn_norm[e], w_out[e])
            new_conv.append(cb)
            new_hgrn.append(S.astype(s_hgrn.dtype))
        else:
            o = l // 2
            y, pb = pool_mixer(h, s_pool[o], start, pool_w[o], pool_scale[o])
            new_pool.append(pb)
        x = x + y
        h = rmsnorm(x, norm_ffn[l])
        y, fb = conv_ffn(h, s_ffn[l], ffn_w_gu[l], ffn_conv_w[l], ffn_conv_b[l], ffn_w_down[l])
        new_ffn.append(fb)
        x = x + y
    out = rmsnorm(x, norm_final)
    return out, jnp.stack(new_conv), jnp.stack(new_hgrn), jnp.stack(new_pool), jnp.stack(new_ffn)


def setup_inputs(seed: int = 0) -> dict:
    key = jax.random.key(seed)
    ks = jax.random.split(key, 24)
    f32 = jnp.float32
    nrm = lambda k, s, sc: (jax.random.normal(k, s, f32) * sc)
    return {
        "x_prompt": nrm(ks[0], (BATCH, SEQ, D_MODEL), 1.0),
        "x_sample": nrm(ks[1], (DEC_BATCH, DEC_SEQ, D_MODEL), 1.0),
        "state_conv_a": nrm(ks[2], (N_EVEN, DEC_BATCH, CONV_W - 1, D_A), 1.0),
        "state_hgrn": nrm(ks[3], (N_EVEN, DEC_BATCH, B_HEADS, B_HEAD_DIM, B_HEAD_DIM), 0.5),
        "state_pool": nrm(ks[4], (N_ODD, DEC_BATCH, POOL_BUF, D_MODEL), 1.0),
        "state_ffn": nrm(ks[5], (DEPTH, DEC_BATCH, FFN_CONV_W - 1, D_FF), 1.0),
        "norm_mix": 1.0 + nrm(ks[6], (DEPTH, D_MODEL), 0.05),
        "norm_ffn": 1.0 + nrm(ks[7], (DEPTH, D_MODEL), 0.05),
        "norm_final": 1.0 + nrm(ks[8], (D_MODEL,), 0.05),
        "w_in": nrm(ks[9], (N_EVEN, D_MODEL, D_IN), D_MODEL ** -0.5),
        "conv_a_w": nrm(ks[10], (N_EVEN, CONV_W, D_A), 0.5),
        "hgrn_lower_bounds": nrm(ks[11], (DEPTH + 1, D_B), 0.5),
        "hgrn_norm": 1.0 + nrm(ks[12], (N_EVEN, D_B), 0.05),
        "w_out": nrm(ks[13], (N_EVEN, D_A + D_B, D_MODEL), (D_A + D_B) ** -0.5),
        "pool_w": nrm(ks[14], (N_ODD, N_POOL_GROUPS, POOL_GROUP, POOL_GROUP), POOL_GROUP ** -0.5),
        "pool_scale": 1.0 + nrm(ks[15], (N_ODD, D_MODEL), 0.1),
        "ffn_w_gu": nrm(ks[16], (DEPTH, D_MODEL, 2 * D_FF), D_MODEL ** -0.5),
        "ffn_conv_w": nrm(ks[17], (DEPTH, FFN_CONV_W, D_FF), 0.5),
        "ffn_conv_b": nrm(ks[18], (DEPTH, D_FF), 0.02),
        "ffn_w_down": nrm(ks[19], (DEPTH, D_FF, D_MODEL), D_FF ** -0.5),
    }


def reference(x_prompt, x_sample, state_conv_a, state_hgrn, state_pool, state_ffn, norm_mix, norm_ffn,
              norm_final, w_in, conv_a_w, hgrn_lower_bounds, hgrn_norm, w_out, pool_w, pool_scale,
              ffn_w_gu, ffn_conv_w, ffn_conv_b, ffn_w_down):
    B = x_prompt.shape[0]
    dt = x_prompt.dtype
    z_conv = jnp.zeros((N_EVEN, B, CONV_W - 1, D_A), dt)
    z_hgrn = jnp.zeros((N_EVEN, B, B_HEADS, B_HEAD_DIM, B_HEAD_DIM), dt)
    z_pool = jnp.zeros((N_ODD, B, POOL_BUF, D_MODEL), dt)
    z_ffn = jnp.zeros((DEPTH, B, FFN_CONV_W - 1, D_FF), dt)
    y_prompt, conv_p, hgrn_p, pool_p, ffn_p = trunk(
        x_prompt, 0, z_conv, z_hgrn, z_pool, z_ffn, norm_mix, norm_ffn, norm_final, w_in, conv_a_w,
        hgrn_lower_bounds, hgrn_norm, w_out, pool_w, pool_scale, ffn_w_gu, ffn_conv_w, ffn_conv_b,
        ffn_w_down)
    y_sample, conv_s, hgrn_s, pool_s, ffn_s = trunk(
        x_sample, PAST_LEN, state_conv_a, state_hgrn, state_pool, state_ffn, norm_mix, norm_ffn,
        norm_final, w_in, conv_a_w, hgrn_lower_bounds, hgrn_norm, w_out, pool_w, pool_scale, ffn_w_gu,
        ffn_conv_w, ffn_conv_b, ffn_w_down)
    return (y_prompt, y_sample, conv_p, conv_s, hgrn_p, hgrn_s, pool_p, pool_s, ffn_p, ffn_s)
```

```python
import contextlib
import numpy as np
import concourse.bass as bass
import concourse.mybir as mybir
from concourse.bass_utils import run_bass_kernel_spmd

F32 = mybir.dt.float32
BF16 = mybir.dt.bfloat16
AF = mybir.ActivationFunctionType
ALU = mybir.AluOpType

SAME_ENGINE_SYNC = True
EMBED_WAITS = True
DEBUG = False
DBG_MAP = {}


class Buf:
    __slots__ = ("name", "w", "r")

    def __init__(self, name):
        self.name = name
        self.w = None
        self.r = []


class Op:
    __slots__ = ("id", "eng", "kind", "fn", "args", "preds", "dur", "lat", "prio", "start", "finish", "idx", "tok", "succs")


def _cost(eng, n):
    if eng == "act":
        return 220 + 0.65 * n
    if eng == "dve":
        return 110 + 1.0 * n
    if eng == "pool":
        return 160 + 1.6 * n
    return 60


class Sched:
    ENGS = ("pe", "act", "dve", "pool", "sp")
    LIST_SCHEDULE = True
    AGE_WEIGHT = 10.0

    def __init__(self, n_dma_slots):
        self.ops = []
        self.nslots = n_dma_slots
        self.bufs = {}
        self.cnt = {e: 0 for e in self.ENGS}

    def B(self, *key):
        b = self.bufs.get(key)
        if b is None:
            b = Buf(key)
            self.bufs[key] = b
        return b

    def _new(self, eng, kind, reads, writes, dur, lat):
        o = Op()
        o.id = len(self.ops)
        o.eng = eng
        o.kind = kind
        o.dur = dur
        o.lat = lat
        preds = {}
        RANK = {"war": 0, "waw": 1, "raw": 2}

        def add(pid, k):
            if pid is None:
                return
            if pid not in preds or RANK[k] > RANK[preds[pid]]:
                preds[pid] = k
        for b in reads:
            add(b.w, "raw")
        for b in writes:
            add(b.w, "waw")
            for rid in b.r:
                if rid != o.id:
                    add(rid, "war")
        o.preds = preds
        for b in reads:
            b.r.append(o.id)
        for b in writes:
            b.w = o.id
            b.r = []
        self.ops.append(o)
        self.cnt[eng] += 1
        return o

    def op(self, eng, fn, reads=(), writes=(), n=512, cost=None):
        dur = cost if cost is not None else _cost(eng, n)
        o = self._new(eng, "op", reads, writes, dur, dur)
        o.fn = fn

    MAX_SWDGE_INFLIGHT = 4

    def dma(self, q, out_ap, in_ap, reads=(), writes=(), nbytes=262144, **kw):
        issue = 1200 if q == "pool" else 120
        o = self._new(q, "dma", reads, writes, issue, issue + 2500 + nbytes / 150.0)
        o.args = (out_ap, in_ap, kw)
        if q == "pool":
            hist = self.__dict__.setdefault("_swdge", [])
            if len(hist) >= self.MAX_SWDGE_INFLIGHT:
                o.preds.setdefault(hist[-self.MAX_SWDGE_INFLIGHT], "raw")
            hist.append(o.id)

    def finish(self, q="sp"):
        self.final_q = q

    def schedule(self):
        ops = self.ops
        N = len(ops)
        for o in ops:
            o.succs = []
        for o in ops:
            for p in o.preds:
                ops[p].succs.append(o.id)
        for o in reversed(ops):
            m = 0.0
            for sid in o.succs:
                if ops[sid].prio > m:
                    m = ops[sid].prio
            o.prio = o.lat + m
        if not self.LIST_SCHEDULE:
            t = 0.0
            for o in ops:
                o.start = t
                t += 1.0
                o.finish = t
            return
        for o in ops:
            o.prio = o.prio - self.AGE_WEIGHT * o.id
        npred = [len(o.preds) for o in ops]
        ready_t = [0.0] * N
        avail = {e: [] for e in self.ENGS}
        for o in ops:
            if npred[o.id] == 0:
                avail[o.eng].append(o.id)
        free = {e: 0.0 for e in self.ENGS}
        done = 0
        XLAT = 300.0
        while done < N:
            best = None
            for e in self.ENGS:
                av = avail[e]
                if not av:
                    continue
                f = free[e]
                mn = min(max(f, ready_t[i]) for i in av)
                pick = None
                for i in av:
                    st = max(f, ready_t[i])
                    if st <= mn + 200.0:
                        if pick is None or ops[i].prio > ops[pick].prio:
                            pick = i
                st = max(f, ready_t[pick])
                if best is None or st < best[0]:
                    best = (st, e, pick)
            st, e, i = best
            o = ops[i]
            avail[e].remove(i)
            o.start = st
            free[e] = st + o.dur
            o.finish = st + o.lat
            done += 1
            for sid in o.succs:
                so = ops[sid]
                t = o.finish + (XLAT if so.eng != e else 40.0)
                if t > ready_t[sid]:
                    ready_t[sid] = t
                npred[sid] -= 1
                if npred[sid] == 0:
                    avail[so.eng].append(sid)
        self.makespan = max(o.finish for o in ops)

    def emit(self, nc, esem, dsem):
        self.schedule()
        ops = self.ops
        order = sorted(ops, key=lambda o: (o.start, o.id))
        cnt = {e: 0 for e in self.ENGS}
        slot_val = [0] * self.nslots
        slot_prev = {}
        nxt = 0
        for o in order:
            if o.kind == "op":
                cnt[o.eng] += 1
                o.tok = (("e", o.eng), cnt[o.eng])
            else:
                k = nxt
                nxt = (nxt + 1) % self.nslots
                if slot_val[k] > 0:
                    slot_prev[o.id] = (("d", k), slot_val[k])
                slot_val[k] += 16
                o.tok = (("d", k), slot_val[k])
        prog = {e: [] for e in self.ENGS}
        seen = {e: {} for e in self.ENGS}
        for o in order:
            need = {}
            sn = seen[o.eng]

            def want(key, val):
                if sn.get(key, 0) >= val:
                    return
                if need.get(key, 0) < val:
                    need[key] = val
            for pid, kind in o.preds.items():
                p = ops[pid]
                if p.kind == "op" and p.eng == o.eng:
                    if o.eng == "pe" or kind == "war" or not SAME_ENGINE_SYNC:
                        continue
                want(p.tok[0], p.tok[1])
            if o.id in slot_prev:
                want(*slot_prev[o.id])
            for k_, v_ in need.items():
                sn[k_] = v_
            prog[o.eng].append((o, need))
        fin = {}
        fq = getattr(self, "final_q", "sp")
        for k in range(self.nslots):
            if slot_val[k] > 0 and seen[fq].get(("d", k), 0) < slot_val[k]:
                fin[("d", k)] = slot_val[k]
        self.prog = prog

        def semof(key):
            return esem[key[1]] if key[0] == "e" else dsem[key[1]]

        class _First:
            def __init__(self, e):
                self._e = e
                self.first = None

            def __getattr__(self, name):
                f = getattr(self._e, name)

                def g(*a, **k):
                    r = f(*a, **k)
                    if self.first is None:
                        self.first = r
                    return r
                return g

        def run(engname, eng):
            embed = EMBED_WAITS and engname != "pe"
            for o, need in prog[engname]:
                items = list(need.items())
                tail = items.pop() if (embed and items) else None
                for key, val in items:
                    eng.wait_ge(semof(key), val)
                if o.kind == "op":
                    px = _First(eng)
                    ins = o.fn(px)
                    if tail is not None:
                        px.first._wait_ge(semof(tail[0]), tail[1])
                    ins.then_inc(esem[engname], 1)
                else:
                    out_ap, in_ap, kw = o.args
                    ins = eng.dma_start(out=out_ap, in_=in_ap, **kw)
                    if tail is not None:
                        ins._wait_ge(semof(tail[0]), tail[1])
                    ins.then_inc(dsem[o.tok[0][1]], 16)
            if engname == fq:
                for key, val in fin.items():
                    eng.wait_ge(semof(key), val)

        with nc.Block() as block:
            @block.tensor
            def _(e):
                run("pe", e)

            @block.scalar
            def _(e):
                run("act", e)

            @block.vector
            def _(e):
                run("dve", e)

            @block.gpsimd
            def _(e):
                run("pool", e)

            @block.sync
            def _(e):
                run("sp", e)


class Ring:
    def __init__(self, S, name, tensors, bufs=None):
        self.t = tensors
        self.b = bufs if bufs is not None else [S.B(name, i) for i in range(len(tensors))]
        self.i = 0

    def next(self):
        k = self.i
        self.i = (k + 1) % len(self.t)
        return self.t[k], self.b[k]


D = 1024
NTL = 1152
DFF = 2816
NJ = 22
EPS = 1e-6
POOL_W = (2, 4, 8, 16)
R_NMIX, R_NFFN, R_NFIN, R_CAW, R_LB, R_HGN, R_PSC, R_FCW, R_FCB = 0, 16, 32, 40, 52, 64, 68, 76, 208
C_AC, C_AB, C_AV, C_BQ, C_BF, C_BI, C_BG = 0, 512, 1024, 1536, 2048, 2560, 3072

NDMA = 24


def build_nc(stop_after=None):
    nc = bass.Bass("TRN2", target_bir_lowering=False)

    def din(name, shape):
        return nc.dram_tensor(name, shape, F32, kind="ExternalInput").ap()

    def dout(name, shape):
        return nc.dram_tensor(name, shape, F32, kind="ExternalOutput").ap()

    xp = din("xp", [2048, D]); xs = din("xs", [128, D])
    stc = din("stc", [32, 512]); sth = din("sth", [16, 4, 128, 128])
    stp = din("stp", [240, D]); stf = din("stf", [2, 32, DFF]); prm = din("prm", [256, 128])
    w_in = din("w_in", [D, 3584]); w_out = din("w_out", [D, D]); w_pool = din("w_pool", [4, 256, 256])
    w_gu = din("w_gu", [2, D, 2 * DFF]); w_dn = din("w_dn", [2, DFF, D])
    y_p = dout("y_p", [2048, D]); y_s = dout("y_s", [128, D])
    conv_p = dout("conv_p", [2, 512]); conv_s = dout("conv_s", [32, 512])
    hg_p = dout("hg_p", [4, 128, 128]); hg_s = dout("hg_s", [16, 4, 128, 128])
    pool_p = dout("pool_p", [15, D]); pool_s = dout("pool_s", [240, D])
    ffn_p = dout("ffn_p", [2, 2, DFF]); ffn_s = dout("ffn_s", [2, 32, DFF])

    S = Sched(NDMA)
    B = S.B
    dbg_d = dout("dbg", [128, 16384]) if DEBUG else None
    dbg_off = [0]

    def dump(name, ap, bufs, n):
        if not DEBUG or name in DBG_MAP:
            return
        off = dbg_off[0]
        dbg_off[0] += n
        DBG_MAP[name] = (off, n)
        S.dma("sp", dbg_d[:, off:off + n], ap, reads=bufs)
    with contextlib.ExitStack() as es:
        _n = [0]

        def sb(shape, dt, name=None):
            _n[0] += 1
            return es.enter_context(nc.sbuf_tensor(name or ("t%d" % _n[0]), shape, dt))

        X = sb([128, 8, NTL], F32, "X")
        H = sb([128, 8, NTL], BF16, "H")
        Y = sb([128, 11, NTL], BF16, "Y")
        WB = Ring(S, "WB", [sb([128, 1408], BF16) for _ in range(8)])
        Fr = Ring(S, "F", [sb([128, 528], F32) for _ in range(13)])
        GXr = Ring(S, "GX", [sb([128, 516], F32) for _ in range(2)])
        Rr = Ring(S, "Rr", [sb([128, 512], F32) for _ in range(2)])
        EGr = Ring(S, "EGr", [sb([128, 16], F32) for _ in range(6)])
        SbR = Ring(S, "SbR", [sb([128, 128], BF16) for _ in range(6)])
        Bh = Ring(S, "Bh", [sb([128, 512], BF16) for _ in range(12)])
        ThR = Ring(S, "ThR", [sb([128, 8], BF16) for _ in range(4)])
        Vall = sb([128, 9, 512], BF16, "Vall")
        XIN = Ring(S, "XIN", [sb([128, 1024], F32) for _ in range(2)])
        _banks = [es.enter_context(nc.psum_tensor("ps%d" % i, [128, 512], F32)) for i in range(8)]
        _bb = [S.B("bank", i) for i in range(8)]
        PS = Ring(S, "PS", _banks, _bb)
        PSg = Ring(S, "PSg", _banks[0:4], _bb[0:4])
        PSO = Ring(S, "PSO", _banks[4:6], _bb[4:6])
        PSC = Ring(S, "PSC", _banks[6:8], _bb[6:8])
        ident = sb([128, 128], F32, "ident")
        onesb = sb([128, 128], BF16, "onesb")
        cmask_p = sb([128, 128], F32, "cmask_p")
        cmask_s = sb([128, 128], F32, "cmask_s")
        seqm = sb([128, 16], F32, "seqm")
        scanm_p = sb([128, 512], F32, "scanm_p")
        scanm_s = sb([128, 128], F32, "scanm_s")
        rcnt = sb([128, 4, 16], F32, "rcnt")
        PT = sb([128, 256], F32, "PT")
        LB = sb([128, 16], F32, "LB")
        epsT = sb([128, 1], F32, "epsT")
        S32 = sb([128, 4, 128], F32, "S32")
        Sb16 = sb([128, 4, 128], BF16, "Sb16")
        S0 = sb([128, 16, 128], F32, "S0")
        S0b = sb([128, 16, 128], BF16, "S0b")
        WI = S0.bitcast(BF16)[:].rearrange("p a b -> p (a b)").rearrange("p (k n) -> p k n", n=512)
        Vblk = sb([128, 16, 128], BF16, "Vblk")
        STCT = sb([128, 4, 32], F32, "STCT")
        STFT = sb([128, NJ, 32], F32, "STFT")
        STPT = sb([128, 8, 240], F32, "STPT")
        UH = sb([128, 4, 2], F32, "UH")
        GH = sb([128, 2, NJ, 2], F32, "GH")
        PH = sb([128, 8, 15], F32, "PH")
        CSS, FSS, PSS = STCT, STFT, STPT
        WP = sb([128, 4, 2, 256], BF16, "WP")

        esem = {e: es.enter_context(nc.semaphore("sem_" + e)) for e in Sched.ENGS}
        dsem = [es.enter_context(nc.semaphore("dsem%d" % i)) for i in range(NDMA)]

        def prow(r):
            return PT[:, r:r + 1]

        BPT = B("PT")
        BID = B("ident")

        def fm_to_rows(src_fn, src_bufs_fn, nch, r, dst):
            for j0 in range(0, nch, 8):
                jn = min(8, nch - j0)
                st, stb = XIN.next()
                for h0 in range(0, jn, 4):
                    hn = min(4, jn - h0)
                    ps, pb = PS.next()

                    def tr(e, ps=ps, j0=j0, h0=h0, hn=hn):
                        ins = None
                        for k in range(hn):
                            ins = e.transpose(out=ps[0:r, k * 128:(k + 1) * 128], in_=src_fn(j0 + h0 + k), identity=ident[:])
                        return ins
                    rd = [BID]
                    for k in range(hn):
                        rd += src_bufs_fn(j0 + h0 + k)
                    S.op("pe", tr, reads=rd, writes=[pb], cost=110 * hn)
                    S.op("act", lambda e, ps=ps, st=st, h0=h0, hn=hn: e.copy(out=st[0:r, h0 * 128:(h0 + hn) * 128], in_=ps[0:r, 0:hn * 128]),
                         reads=[pb], writes=[stb], n=hn * 128)
                S.dma("sp", dst[:, j0 * 128:(j0 + jn) * 128], st[0:r, 0:jn * 128], reads=[stb])

        def rows_to_fm(src, r, nch, dst_fn, dst_bufs_fn):
            for j0 in range(0, nch, 8):
                jn = min(8, nch - j0)
                st, stb = XIN.next()
                S.dma("sp", st[0:r, 0:jn * 128], src[:, j0 * 128:(j0 + jn) * 128], writes=[stb])
                per = max(1, 512 // r)
                k = 0
                while k < jn:
                    kn = min(per, jn - k)
                    ps, pb = PS.next()

                    def tr(e, ps=ps, st=st, k=k, kn=kn):
                        ins = None
                        for q in range(kn):
                            ins = e.transpose(out=ps[:, q * r:(q + 1) * r], in_=st[0:r, (k + q) * 128:(k + q + 1) * 128], identity=ident[0:r, 0:r])
                        return ins
                    S.op("pe", tr, reads=[stb, BID], writes=[pb], cost=110 * kn)
                    for q in range(kn):
                        jq = j0 + k + q
                        S.op("act", lambda e, ps=ps, q=q, jq=jq: e.copy(out=dst_fn(jq), in_=ps[:, q * r:(q + 1) * r]),
                             reads=[pb], writes=dst_bufs_fn(jq), n=r)
                    k += kn

        class WStream:
            def __init__(self, specs, depth=3):
                self.specs = specs
                self.i = 0
                self.depth = depth
                self.ready = []
                self._fill(depth)

            def _fill(self, k):
                while len(self.ready) < k and self.i < len(self.specs):
                    src, kc, ncols = self.specs[self.i]
                    self.i += 1
                    wb, wbb = WB.next()
                    n = kc * ncols
                    view = wb[:, 0:n].rearrange("p (k n) -> p k n", n=ncols)
                    S.dma("pool", view, src, writes=[wbb])
                    self.ready.append((view, wbb))

            def get_n(self, n):
                assert n + (self.depth if self.i + max(0, n - len(self.ready)) < len(self.specs) else 0) <= len(WB.t)
                self._fill(n)
                out = [self.ready.pop(0) for _ in range(n)]
                self._fill(self.depth)
                return out

            def get(self):
                return self.get_n(1)[0]

        def wsrc(w2d, col0, ncols=128):
            return (w2d[:, col0:col0 + ncols].rearrange("(k p) n -> p k n", p=128), w2d.shape[0] // 128, ncols)

        def mm_fm(ps, pb, w, wbuf, src, src_bufs, t0, n, nk):
            def f(e):
                ins = None
                for k in range(nk):
                    ins = e.matmul(ps[:, 0:n], lhsT=w[:, k, :], rhs=src[:, k, t0:t0 + n], start=(k == 0), stop=(k == nk - 1))
                return ins
            S.op("pe", f, reads=[wbuf] + src_bufs, writes=[pb], cost=nk * (0.51 * n + 22))

        def BX(b):
            return [B("X", c, b) for c in range(8)]

        def BH(b):
            return [B("H", c, b) for c in range(8)]

        def BY(b):
            return [B("Y", b)]

        def setup():
            S.op("pool", lambda e: e.memset(ident[:], 1.0), writes=[BID])
            S.op("pool", lambda e: e.affine_select(out=ident[:], in_=ident[:], pattern=[[-1, 128]], compare_op=ALU.is_equal, fill=0.0,
                                                   base=0, channel_multiplier=1), reads=[BID], writes=[BID])

            def P1(fn, writes, reads=()):
                S.op("pool", fn, reads=list(reads), writes=list(writes))
            P1(lambda e: e.memset(onesb[:], 1.0), [B("c_ones")])
            P1(lambda e: e.memset(epsT[:], EPS), [B("c_eps")])
            P1(lambda e: e.memset(cmask_p[:], 1.0), [B("c_cmp")])
            P1(lambda e: e.affine_select(out=cmask_p[:], in_=cmask_p[:], pattern=[[1, 128]], compare_op=ALU.is_ge, fill=0.0, base=0, channel_multiplier=-1), [B("c_cmp")], [B("c_cmp")])
            P1(lambda e: e.memset(cmask_p[0:64, 64:128], 0.0), [B("c_cmp")], [B("c_cmp")])
            P1(lambda e: e.memset(cmask_s[:], 1.0), [B("c_cms")])
            P1(lambda e: e.affine_select(out=cmask_s[:], in_=cmask_s[:], pattern=[[1, 128]], compare_op=ALU.is_ge, fill=0.0, base=0, channel_multiplier=-1), [B("c_cms")], [B("c_cms")])
            cm3 = cmask_s[:].rearrange("p (n t) -> p n t", t=8)
            P1(lambda e: e.affine_select(out=cm3, in_=cm3, pattern=[[-8, 16], [0, 8]], compare_op=ALU.is_ge, fill=0.0, base=0, channel_multiplier=1), [B("c_cms")], [B("c_cms")])
            P1(lambda e: e.memset(seqm[:], 1.0), [B("c_seqm")])
            P1(lambda e: e.affine_select(out=seqm[:], in_=seqm[:], pattern=[[-8, 16]], compare_op=ALU.is_ge, fill=0.0, base=0, channel_multiplier=1), [B("c_seqm")], [B("c_seqm")])
            P1(lambda e: e.affine_select(out=seqm[:], in_=seqm[:], pattern=[[8, 16]], compare_op=ALU.is_ge, fill=0.0, base=7, channel_multiplier=-1), [B("c_seqm")], [B("c_seqm")])
            P1(lambda e: e.memset(scanm_p[:], 1.0), [B("c_scp")])
            P1(lambda e: e.memset(scanm_p[:].rearrange("p (c t) -> p c t", t=64)[:, :, 0:1], 0.0), [B("c_scp")], [B("c_scp")])
            P1(lambda e: e.memset(scanm_s[:], 1.0), [B("c_scs")])
            P1(lambda e: e.memset(scanm_s[:].rearrange("p (c t) -> p c t", t=8)[:, :, 0:1], 0.0), [B("c_scs")], [B("c_scs")])
            for gi, w in enumerate(POOL_W):
                P1(lambda e, gi=gi, w=w: e.memset(rcnt[:, gi, :], 1.0), [B("c_rcnt")], [B("c_rcnt")])
                for t in range(w - 1):
                    P1(lambda e, gi=gi, t=t, w=w: e.memset(rcnt[:, gi, t:t + 1], float(w) / (t + 1)), [B("c_rcnt")], [B("c_rcnt")])
            P1(lambda e: e.memset(UH[:], 0.0), [B("UH", c) for c in range(4)])
            P1(lambda e: e.memset(GH[:], 0.0), [B("GH", l, j) for l in range(2) for j in range(NJ)])
            P1(lambda e: e.memset(PH[:], 0.0), [B("PH", c) for c in range(8)])
            P1(lambda e: e.memset(S32[:], 0.0), [B("S32", h) for h in range(4)])
            P1(lambda e: e.memset(epsT[:], EPS), [B("consts")], [B("c_ones"), B("c_eps"), B("c_cmp"), B("c_cms"), B("c_seqm"), B("c_scp"), B("c_scs"), B("c_rcnt")])
            S.op("pool", lambda e: e.tensor_copy(out=Sb16[:], in_=S32[:]), reads=[B("S32", h) for h in range(4)], writes=[B("Sb16", h) for h in range(4)])
            st, stb = XIN.next()
            S.dma("sp", st[:, 0:256].rearrange("p (a c) -> p a c", c=128), prm.rearrange("(a p) c -> p a c", p=128), writes=[stb])
            ps, pb = PS.next()

            def trp(e, ps=ps, st=st):
                e.transpose(out=ps[:, 0:128], in_=st[:, 0:128], identity=ident[:])
                return e.transpose(out=ps[:, 128:256], in_=st[:, 128:256], identity=ident[:])
            S.op("pe", trp, reads=[stb, BID], writes=[pb], cost=220)
            S.op("act", lambda e, ps=ps: e.copy(out=PT[:], in_=ps[:, 0:256]), reads=[pb], writes=[BPT], n=256)
            Le, Leb = Fr.next()
            S.op("act", lambda e: e.activation(out=Le[:, 0:12], in_=PT[:, R_LB:R_LB + 12], func=AF.Exp), reads=[BPT], writes=[Leb])

            BLe = Leb
            D1 = lambda fn: S.op("dve", fn, reads=[BLe, B("LB")], writes=[BLe, B("LB")])
            D1(lambda e: e.tensor_tensor(out=Le[:, 12:16], in0=Le[:, 0:4], in1=Le[:, 4:8], op=ALU.add))
            D1(lambda e: e.tensor_tensor(out=Le[:, 12:16], in0=Le[:, 12:16], in1=Le[:, 8:12], op=ALU.add))
            D1(lambda e: e.reciprocal(out=Le[:, 12:16], in_=Le[:, 12:16]))
            D1(lambda e: e.tensor_tensor(out=LB[:, 0:4], in0=Le[:, 0:4], in1=Le[:, 12:16], op=ALU.mult))
            D1(lambda e: e.tensor_scalar(out=LB[:, 4:8], in0=LB[:, 0:4], scalar1=-1.0, scalar2=1.0, op0=ALU.mult, op1=ALU.add))
            D1(lambda e: e.tensor_scalar(out=LB[:, 8:12], in0=LB[:, 0:4], scalar1=-1.0, scalar2=0.0, op0=ALU.add, op1=ALU.add))
            dump("LB", LB[:, 0:16], [B("LB")], 16)
            dump("PT", PT[:, 0:256], [BPT], 256)

        def late_setup():
            rows_to_fm(stc, 32, 4, lambda j: STCT[:, j, :], lambda j: [B("STCT", j)])
            for h in range(2):
                rows_to_fm(stp[h * 120:(h + 1) * 120, :], 120, 8, lambda j, h=h: STPT[:, j, h * 120:(h + 1) * 120], lambda j: [B("STPT", j)])


        BC = B("consts")

        def load_x(tiles):
            for src, t0, b in tiles:
                for half in range(2):
                    st, stb = Fr.next()
                    S.dma("sp", st[:, 0:512], src[:, half * 512:(half + 1) * 512], writes=[stb], nbytes=2 ** 18)
                    ps, pb = PS.next()

                    def tr(e, ps=ps, st=st):
                        ins = None
                        for q in range(4):
                            ins = e.transpose(out=ps[:, q * 128:(q + 1) * 128], in_=st[:, q * 128:(q + 1) * 128], identity=ident[:])
                        return ins
                    S.op("pe", tr, reads=[stb, BID], writes=[pb], cost=440)
                    eng = "act" if half == 0 else "dve"

                    def cp(e, ps=ps, half=half, t0=t0, eng=eng):
                        o = X[:, half * 4:half * 4 + 4, t0:t0 + 128]
                        i = ps[:, :].rearrange("p (q t) -> p q t", t=128)
                        if eng == "act":
                            return e.copy(out=o, in_=i)
                        return e.tensor_copy(out=o, in_=i)
                    S.op(eng, cp, reads=[pb], writes=[B("X", c, b) for c in range(half * 4, half * 4 + 4)])

        def norm_R(b, t0, n, split=False):
            ps, pb = PS.next()
            for c in range(8):
                sq, sqb = Bh.next()
                if not (split and c % 2 == 1):
                    S.op("act", lambda e, sq=sq, c=c: e.activation(out=sq[:, 0:n], in_=X[:, c, t0:t0 + n], func=AF.Square),
                         reads=[B("X", c, b)], writes=[sqb])
                else:
                    S.op("dve", lambda e, sq=sq, c=c: e.tensor_tensor(out=sq[:, 0:n], in0=X[:, c, t0:t0 + n], in1=X[:, c, t0:t0 + n], op=ALU.mult),
                         reads=[B("X", c, b)], writes=[sqb])
                S.op("pe", lambda e, ps=ps, sq=sq, c=c: e.matmul(ps[:, 0:n], lhsT=onesb[:], rhs=sq[:, 0:n], start=(c == 0), stop=(c == 7)),
                     reads=[sqb, BC], writes=[pb], cost=0.51 * n + 22)
            R, Rb = Rr.next()
            S.op("act", lambda e: e.activation(out=R[:, 0:n], in_=ps[:, 0:n], func=AF.Ln, bias=epsT[:, 0:1], scale=1.0 / D),
                 reads=[pb, BC], writes=[Rb])
            S.op("act", lambda e: e.activation(out=R[:, 0:n], in_=R[:, 0:n], func=AF.Exp, scale=-0.5), reads=[Rb], writes=[Rb])
            return R, Rb

        def norm_to_H(blocks, grow):
            for (b, t0, n, kind) in blocks:
                R, Rb = norm_R(b, t0, n, split=True)
                for c in range(8):
                    S.op("dve", lambda e, c=c, R=R, t0=t0, n=n: e.scalar_tensor_tensor(
                        out=H[:, c, t0:t0 + n], in0=X[:, c, t0:t0 + n], scalar=prow(grow + c), in1=R[:, 0:n], op0=ALU.mult, op1=ALU.mult),
                        reads=[B("X", c, b), Rb, BPT], writes=[B("H", c, b)])

        def ext_views(buf, n, kind, halo):
            if kind == "p":
                return (buf[:, halo:halo + n], [buf[:, k:k + n] for k in range(halo + 1)], buf[:, 0:halo], buf[:, n:n + halo])
            L = halo + 8
            v = buf[:, 0:16 * L].rearrange("p (s l) -> p s l", l=L)
            return (v[:, :, halo:L], [v[:, :, k:k + 8] for k in range(halo + 1)], v[:, :, 0:halo], v[:, :, 8:L])

        def v3(ap, kind):
            return ap if kind == "p" else ap.rearrange("p (s l) -> p s l", l=8)

        def mixer_a(blocks, W, cs=range(4)):
            for c in cs:
                (wc, wcb), (wv, wvb), (wbm, wbb) = W.get_n(3)
                for (b, t0, n, kind) in blocks:
                    psc, pscb = PSg.next(); mm_fm(psc, pscb, wc, wcb, H, BH(b), t0, n, 8)
                    psv, psvb = PSg.next(); mm_fm(psv, psvb, wv, wvb, H, BH(b), t0, n, 8)
                    psb, psbb = PSg.next(); mm_fm(psb, psbb, wbm, wbb, H, BH(b), t0, n, 8)
                    T1, T1b = Fr.next()
                    S.op("act", lambda e, T1=T1, psc=psc, n=n: e.copy(out=T1[:, 0:n], in_=psc[:, 0:n]), reads=[pscb], writes=[T1b])
                    U, Ub = GXr.next()
                    data, taps, halo_v, tail_v = ext_views(U, n, kind, 2)
                    if kind == "p":
                        hsrc, hb = UH[:, c, :], B("UH", c)
                        tdst, tb = UH[:, c, :], B("UH", c)
                    else:
                        hsrc, hb = STCT[:, c, :].rearrange("p (s j) -> p s j", j=2), B("STCT", c)
                        tdst, tb = CSS[:, c, :].rearrange("p (s j) -> p s j", j=2), B("STCT", c)
                    S.op("pool", lambda e, halo_v=halo_v, hsrc=hsrc: e.tensor_copy(out=halo_v, in_=hsrc), reads=[hb], writes=[Ub], n=16)
                    S.op("dve", lambda e, data=data, T1=T1, psv=psv, n=n, kind=kind: e.tensor_tensor(
                        out=data, in0=v3(T1[:, 0:n], kind), in1=v3(psv[:, 0:n], kind), op=ALU.mult), reads=[T1b, psvb], writes=[Ub])
                    S.op("pool", lambda e, tail_v=tail_v, tdst=tdst: e.tensor_copy(out=tdst, in_=tail_v), reads=[Ub], writes=[tb], n=16)
                    A0, A0b = Fr.next()
                    a0 = v3(A0[:, 0:n], kind)
                    S.op("act", lambda e, a0=a0, taps=taps, c=c: e.activation(
                        out=a0, in_=taps[2], func=AF.Identity, scale=prow(R_CAW + 8 + c)), reads=[Ub, BPT], writes=[A0b])
                    for k in (1, 0):
                        S.op("dve", lambda e, a0=a0, taps=taps, c=c, k=k: e.scalar_tensor_tensor(
                            out=a0, in0=taps[k], scalar=prow(R_CAW + 4 * k + c), in1=a0, op0=ALU.mult, op1=ALU.add), reads=[Ub, A0b, BPT], writes=[A0b])
                    S.op("dve", lambda e, A0=A0, psb=psb, c=c, t0=t0, n=n: e.tensor_tensor(
                        out=Y[:, c, t0:t0 + n], in0=A0[:, 0:n], in1=psb[:, 0:n], op=ALU.mult), reads=[A0b, psbb], writes=[B("Y", c, b)])

        def hgrn(blocks, W, last_sb, hds=range(4), prologue=True):
            has_s = any(k == "s" for (_, _, _, k) in blocks)
            if prologue:
                hgrn_prologue(blocks)
            hgrn_heads(blocks, W, last_sb, hds, has_s)

        def hgrn_prologue(blocks):
            for k2 in range(4):
                S.dma("pool", WI[:, 2 * k2:2 * k2 + 2, :], w_in[256 * k2:256 * k2 + 256, C_BI:C_BI + 512].rearrange("(k p) n -> p k n", p=128),
                      reads=[B("S0")] if k2 else [], writes=[B("S0")], nbytes=2 ** 19)
            for (b, t0, n, kind) in blocks:
                for tt in range(n // 128):
                    gt = (t0 + tt * 128) // 128
                    psv, psvb = PSg.next()

                    def fv(e, psv=psv, t0=t0, tt=tt):
                        ins = None
                        for k in range(8):
                            ins = e.matmul(psv[:, 0:512], lhsT=H[:, k, t0 + tt * 128:t0 + tt * 128 + 128], rhs=WI[:, k, :], start=(k == 0), stop=(k == 7))
                        return ins
                    S.op("pe", fv, reads=[B("S0")] + BH(b), writes=[psvb], cost=8 * 283)
                    S.op("act", lambda e, psv=psv, gt=gt: e.copy(out=Vall[:, gt, :], in_=psv[:, 0:512]), reads=[psvb], writes=[B("Vall", gt)])

        def hgrn_heads(blocks, W, last_sb, hds, has_s):
            for hd in hds:
                (wq, wqb), (wf, wfb), (wg, wgb) = W.get_n(3)
                if has_s:
                    S.dma("sp", S0[:], sth[:, hd].rearrange("n d e -> d n e"), writes=[B("S0")])
                    S.op("act", lambda e: e.copy(out=S0b[:], in_=S0[:]), reads=[B("S0")], writes=[B("S0b")], n=2048)
                for (b, t0, n, kind) in blocks:
                    ch = 64 if kind == "p" else 8
                    scanm = scanm_p if kind == "p" else scanm_s
                    cmask = cmask_p if kind == "p" else cmask_s
                    psq, psqb = PSg.next(); mm_fm(psq, psqb, wq, wqb, H, BH(b), t0, n, 8)
                    psf, psfb = PSg.next(); mm_fm(psf, psfb, wf, wfb, H, BH(b), t0, n, 8)
                    psg, psgb = PSg.next(); mm_fm(psg, psgb, wg, wgb, H, BH(b), t0, n, 8)
                    BLB = B("LB")
                    nchk = n // ch
                    fs, fsb = Fr.next()
                    S.op("act", lambda e, fs=fs, psg=psg, n=n: e.activation(out=fs[:, 0:n], in_=psg[:, 0:n], func=AF.Exp, scale=-1.0), reads=[psgb], writes=[fsb])
                    S.op("act", lambda e, fs=fs, n=n: e.activation(out=fs[:, 0:n], in_=fs[:, 0:n], func=AF.Ln, bias=1.0), reads=[fsb], writes=[fsb])
                    S.op("act", lambda e, fs=fs, n=n: e.activation(out=fs[:, 0:n], in_=fs[:, 0:n], func=AF.Exp, scale=-1.0), reads=[fsb], writes=[fsb])
                    SGb, SGbb = Bh.next()
                    S.op("dve", lambda e, SGb=SGb, fs=fs, psg=psg, n=n: e.tensor_tensor(out=SGb[:, 0:n], in0=psg[:, 0:n], in1=fs[:, 0:n], op=ALU.mult), reads=[psgb, fsb], writes=[SGbb])
                    fq, fqb = Fr.next()
                    S.op("act", lambda e, fq=fq, psq=psq, n=n: e.copy(out=fq[:, 0:n], in_=psq[:, 0:n]), reads=[psqb], writes=[fqb])
                    f0, f0b = Fr.next(); f1, f1b = Fr.next(); f2, f2b = Fr.next(); f3, f3b = Fr.next()
                    S.op("act", lambda e, f0=f0, psf=psf, n=n: e.activation(out=f0[:, 0:n], in_=psf[:, 0:n], func=AF.Exp, scale=-1.0), reads=[psfb], writes=[f0b])
                    S.op("act", lambda e, f0=f0, n=n: e.activation(out=f0[:, 0:n], in_=f0[:, 0:n], func=AF.Ln, bias=1.0), reads=[f0b], writes=[f0b])
                    S.op("act", lambda e, f0=f0, n=n: e.activation(out=f0[:, 0:n], in_=f0[:, 0:n], func=AF.Exp, scale=-1.0), reads=[f0b], writes=[f0b])
                    S.op("act", lambda e, f0=f0, f1=f1, n=n, hd=hd: e.activation(
                        out=f1[:, 0:n], in_=f0[:, 0:n], func=AF.Identity, bias=LB[:, 4 + hd:5 + hd], scale=LB[:, 8 + hd:9 + hd]),
                        reads=[f0b, BLB], writes=[f1b])
                    S.op("act", lambda e, f0=f0, n=n, hd=hd: e.activation(
                        out=f0[:, 0:n], in_=f0[:, 0:n], func=AF.Ln, bias=LB[:, hd:hd + 1], scale=LB[:, 4 + hd:5 + hd]), reads=[f0b, BLB], writes=[f0b])
                    S.op("dve", lambda e, f0=f0, f2=f2, n=n, scanm=scanm: e.tensor_tensor_scan(
                        out=f2[:, 0:n], data0=scanm[:, 0:n], data1=f0[:, 0:n], initial=0.0, op0=ALU.mult, op1=ALU.add), reads=[f0b, BC], writes=[f2b], n=2 * n)
                    S.op("act", lambda e, f0=f0, f2=f2, n=n: e.activation(out=f0[:, 0:n], in_=f2[:, 0:n], func=AF.Exp), reads=[f2b], writes=[f0b])
                    S.op("act", lambda e, f3=f3, f2=f2, n=n: e.activation(out=f3[:, 0:n], in_=f2[:, 0:n], func=AF.Exp, scale=-1.0), reads=[f2b], writes=[f3b])
                    egl_t, eglb = EGr.next()
                    eg3 = f0[:, 0:n].rearrange("p (c t) -> p c t", t=ch)
                    S.op("act", lambda e, egl_t=egl_t, eg3=eg3, nchk=nchk, ch=ch: e.copy(out=egl_t[:, 0:nchk].unsqueeze(2), in_=eg3[:, :, ch - 1:ch]),
                         reads=[f0b], writes=[eglb], n=16)
                    QT, QTb = Bh.next()
                    S.op("dve", lambda e, QT=QT, fq=fq, f0=f0, n=n: e.tensor_tensor(out=QT[:, 0:n], in0=fq[:, 0:n], in1=f0[:, 0:n], op=ALU.mult),
                         reads=[fqb, f0b], writes=[QTb])
                    KTb, KTbb = Bh.next()
                    S.op("dve", lambda e, KTb=KTb, f1=f1, f3=f3, n=n: e.tensor_tensor(out=KTb[:, 0:n], in0=f1[:, 0:n], in1=f3[:, 0:n], op=ALU.mult),
                         reads=[f1b, f3b], writes=[KTbb])
                    S.op("pool", lambda e, f1=f1, f3=f3, n=n: e.tensor_tensor(out=f1[:, 0:n], in0=f1[:, 0:n], in1=f3[:, 0:n], op=ALU.mult),
                         reads=[f1b, f3b], writes=[f1b])
                    S.op("dve", lambda e, f3=f3, f1=f1, egl_t=egl_t, n=n, ch=ch, nchk=nchk: e.tensor_tensor(
                        out=f3[:, 0:n].rearrange("p (c t) -> p c t", t=ch), in0=f1[:, 0:n].rearrange("p (c t) -> p c t", t=ch),
                        in1=egl_t[:, 0:nchk].unsqueeze(2).broadcast_to([128, nchk, ch]), op=ALU.mult), reads=[f1b, eglb], writes=[f3b])
                    ntile = n // 128
                    gt0 = t0 // 128
                    BV = [B("Vall", gt0 + tt) for tt in range(ntile)]

                    def Vt(tt, hd=hd, gt0=gt0):
                        return Vall[:, gt0 + tt, hd * 128:(hd + 1) * 128]
                    pskh, pskhb = PSg.next()

                    def ftr(e, pskh=pskh, f3=f3, ntile=ntile):
                        ins = None
                        for tt in range(ntile):
                            ins = e.transpose(out=pskh[:, tt * 128:(tt + 1) * 128], in_=f3[:, tt * 128:(tt + 1) * 128], identity=ident[:])
                        return ins
                    S.op("pe", ftr, reads=[f3b, BID], writes=[pskhb], cost=130 * ntile)
                    KHt, KHtb = Bh.next()
                    S.op("act", lambda e, KHt=KHt, pskh=pskh, n=n: e.copy(out=KHt[:, 0:n], in_=pskh[:, 0:n]), reads=[pskhb], writes=[KHtb], n=n)
                    psa, psab = PSg.next()

                    def fat(e, psa=psa, KTb=KTb, QT=QT, ntile=ntile):
                        ins = None
                        for tt in range(ntile):
                            sl = slice(tt * 128, (tt + 1) * 128)
                            ins = e.matmul(psa[:, sl], lhsT=KTb[:, sl], rhs=QT[:, sl], start=True, stop=True)
                        return ins
                    S.op("pe", fat, reads=[KTbb, QTb], writes=[psab], cost=110 * ntile)
                    ATm, ATmb = Bh.next()
                    S.op("dve", lambda e, ATm=ATm, psa=psa, cmask=cmask, n=n, ntile=ntile: e.tensor_tensor(
                        out=ATm[:, 0:n].rearrange("p (a t) -> p a t", t=128), in0=psa[:, 0:n].rearrange("p (a t) -> p a t", t=128),
                        in1=cmask[:].unsqueeze(1).broadcast_to([128, ntile, 128]), op=ALU.mult), reads=[psab, BC], writes=[ATmb], n=n)
                    pso, psob = PSO.next()

                    def fin(e, pso=pso, ATm=ATm, ntile=ntile, Vt=Vt):
                        ins = None
                        for tt in range(ntile):
                            sl = slice(tt * 128, (tt + 1) * 128)
                            ins = e.matmul(pso[:, sl], lhsT=Vt(tt), rhs=ATm[:, sl], start=(tt == 0), stop=False, skip_group_check=True)
                        return ins
                    S.op("pe", fin, reads=[ATmb] + BV, writes=[psob], cost=110 * ntile)
                    if kind == "p":
                        steps = [(tt, cki) for tt in range(ntile) for cki in range(2)]
                        pbank = {}
                        for cki_g in range(2):
                            grp = [(tt, cki_g) for tt in range(ntile)]
                            pss, pssb = PSC.next()

                            def fpp(e, pss=pss, grp=grp, KHt=KHt, Vt=Vt):
                                ins = None
                                for q, (tt, cki) in enumerate(grp):
                                    lo = cki * 64
                                    ins = e.matmul(pss[:, q * 128:(q + 1) * 128], lhsT=KHt[lo:lo + 64, tt * 128:(tt + 1) * 128], rhs=Vt(tt)[lo:lo + 64, :], start=True, stop=True)
                                return ins
                            S.op("pe", fpp, reads=[KHtb] + BV, writes=[pssb], cost=120 * len(grp))
                            for q, st_ in enumerate(grp):
                                pbank[st_] = (pss[:, q * 128:(q + 1) * 128], pssb)
                        cur, curb = Sb16[:, hd, :], B("Sb16", hd)
                        for ci, (tt, cki) in enumerate(steps):
                            c0 = tt * 128 + cki * 64
                            S.op("pe", lambda e, pso=pso, cur=cur, QT=QT, c0=c0: e.matmul(
                                pso[:, c0:c0 + 64], lhsT=cur, rhs=QT[:, c0:c0 + 64], start=False, stop=True, skip_group_check=True),
                                reads=[curb, QTb], writes=[psob], cost=110)
                            pc, pcb = pbank[(tt, cki)]
                            if ci == len(steps) - 1:
                                nxt, nxtb = Sb16[:, hd, :], B("Sb16", hd)
                            else:
                                nxt, nxtb = SbR.next()
                                nxt = nxt[:]
                            S.op("dve", lambda e, nxt=nxt, pc=pc, egl_t=egl_t, ci=ci, hd=hd: e.scalar_tensor_tensor(
                                out=nxt, in0=S32[:, hd, :], scalar=egl_t[:, ci:ci + 1], in1=pc, op0=ALU.mult, op1=ALU.add),
                                reads=[B("S32", hd), eglb, pcb], writes=[nxtb], n=128)
                            S.op("dve", lambda e, pc=pc, egl_t=egl_t, ci=ci, hd=hd: e.scalar_tensor_tensor(
                                out=S32[:, hd, :], in0=S32[:, hd, :], scalar=egl_t[:, ci:ci + 1], in1=pc, op0=ALU.mult, op1=ALU.add),
                                reads=[B("S32", hd), eglb, pcb], writes=[B("S32", hd)], n=128)
                            cur, curb = nxt, nxtb
                    else:
                        def fo(e, pso=pso, QT=QT):
                            ins = None
                            for s_ in range(16):
                                ins = e.matmul(pso[:, 8 * s_:8 * s_ + 8], lhsT=S0b[:, s_, :], rhs=QT[:, 8 * s_:8 * s_ + 8], start=False, stop=True, skip_group_check=True)
                            return ins
                        S.op("pe", fo, reads=[B("S0b"), QTb], writes=[psob], cost=1100)
                        S.op("dve", lambda e, Vt=Vt: e.tensor_tensor(
                            out=Vblk[:], in0=Vt(0).unsqueeze(1).broadcast_to([128, 16, 128]), in1=seqm[:].unsqueeze(2).broadcast_to([128, 16, 128]), op=ALU.mult),
                            reads=[BV[0], BC], writes=[B("Vblk")], n=2048)
                        egl = egl_t[:, 0:16].unsqueeze(2)
                        for j4 in range(4):
                            pss, pssb = PSC.next()
                            S.op("pe", lambda e, pss=pss, KHt=KHt, j4=j4: e.matmul(
                                pss[:, 0:512], lhsT=KHt[:, 0:128], rhs=Vblk[:, 4 * j4:4 * j4 + 4, :].rearrange("p s e -> p (s e)"), start=True, stop=True),
                                reads=[KHtb, B("Vblk")], writes=[pssb], cost=283)
                            S.op("pool", lambda e, j4=j4, egl=egl: e.tensor_tensor(
                                out=S0[:, 4 * j4:4 * j4 + 4, :], in0=S0[:, 4 * j4:4 * j4 + 4, :], in1=egl[:, 4 * j4:4 * j4 + 4, :].broadcast_to([128, 4, 128]), op=ALU.mult),
                                reads=[B("S0"), eglb], writes=[B("S0")])
                            S.op("dve", lambda e, pss=pss, j4=j4: e.tensor_tensor(
                                out=S0[:, 4 * j4:4 * j4 + 4, :], in0=S0[:, 4 * j4:4 * j4 + 4, :], in1=pss[:, 0:512].rearrange("p (s e) -> p s e", e=128), op=ALU.add),
                                reads=[B("S0"), pssb], writes=[B("S0")])
                        S.dma("sp", hg_s[:, hd].rearrange("n d e -> d n e"), S0[:], reads=[B("S0")], nbytes=2 ** 20)
                    f4, f4b = Fr.next()
                    S.op("act", lambda e, f4=f4, pso=pso, n=n: e.copy(out=f4[:, 0:n], in_=pso[:, 0:n]), reads=[psob], writes=[f4b], n=n)
                    OSQ, OSQb = Bh.next()
                    S.op("act", lambda e, OSQ=OSQ, pso=pso, n=n: e.activation(out=OSQ[:, 0:n], in_=pso[:, 0:n], func=AF.Square), reads=[psob], writes=[OSQb], n=n)
                    psn, psnb = PSg.next()
                    S.op("pe", lambda e, psn=psn, OSQ=OSQ, n=n: e.matmul(psn[:, 0:n], lhsT=onesb[:], rhs=OSQ[:, 0:n], start=True, stop=True), reads=[OSQb, BC], writes=[psnb], cost=0.51 * n + 22)
                    fr, frb = Fr.next()
                    S.op("act", lambda e, fr=fr, psn=psn, n=n: e.activation(out=fr[:, 0:n], in_=psn[:, 0:n], func=AF.Ln, bias=epsT[:, 0:1], scale=1.0 / 128),
                         reads=[psnb, BC], writes=[frb])
                    S.op("act", lambda e, fr=fr, n=n: e.activation(out=fr[:, 0:n], in_=fr[:, 0:n], func=AF.Exp, scale=-0.5), reads=[frb], writes=[frb])
                    S.op("dve", lambda e, f4=f4, fr=fr, n=n, hd=hd: e.scalar_tensor_tensor(
                        out=f4[:, 0:n], in0=f4[:, 0:n], scalar=prow(R_HGN + hd), in1=fr[:, 0:n], op0=ALU.mult, op1=ALU.mult), reads=[f4b, frb, BPT], writes=[f4b])
                    S.op("dve", lambda e, f4=f4, SGb=SGb, n=n, t0=t0, hd=hd: e.tensor_tensor(
                        out=Y[:, 4 + hd, t0:t0 + n], in0=f4[:, 0:n], in1=SGb[:, 0:n], op=ALU.mult), reads=[f4b, SGbb], writes=[B("Y", 4 + hd, b)])
            if last_sb and list(hds)[-1] == 3:
                S.dma("sp", hg_p.rearrange("h d e -> d h e"), S32[:], reads=[B("S32", h) for h in range(4)])

        def w_out_phase(blocks, W):
            for oc in range(8):
                w, wb = W.get()
                for (b, t0, n, kind) in blocks:
                    ps, pb = PS.next()
                    mm_fm(ps, pb, w, wb, Y, [B("Y", c, b) for c in range(8)], t0, n, 8)
                    S.op("dve", lambda e, ps=ps, oc=oc, t0=t0, n=n: e.tensor_tensor(
                        out=X[:, oc, t0:t0 + n], in0=ps[:, 0:n], in1=X[:, oc, t0:t0 + n], op=ALU.add), reads=[pb, B("X", oc, b)], writes=[B("X", oc, b)])

        def ffn(l, blocks, last_sb):
            has_s = any(k == "s" for (_, _, _, k) in blocks)
            norm_to_H(blocks, R_NFFN + 8 * l)
            if has_s:
                rows_to_fm(stf[l], 32, NJ, lambda j: STFT[:, j, :], lambda j: [B("STFT", j)])
            for half in range(2):
                specs = []
                for jj in range(11):
                    j = half * 11 + jj
                    specs += [wsrc(w_gu[l], j * 128), wsrc(w_gu[l], DFF + j * 128)]
                for oc in range(8):
                    specs.append((w_dn[l, half * 1408:(half + 1) * 1408, oc * 128:(oc + 1) * 128].rearrange("(k p) n -> p k n", p=128), 11, 128))
                W = WStream(specs)
                for jj in range(11):
                    j = half * 11 + jj
                    (wg, wgb), (wu, wub) = W.get_n(2)
                    for (b, t0, n, kind) in blocks:
                        psg, psgb = PS.next(); mm_fm(psg, psgb, wg, wgb, H, BH(b), t0, n, 8)
                        psu, psub = PS.next(); mm_fm(psu, psub, wu, wub, H, BH(b), t0, n, 8)
                        GX, GXb = GXr.next()
                        data, taps, halo_v, tail_v = ext_views(GX, n, kind, 2)
                        if kind == "p":
                            hsrc, hb = GH[:, l, j, :], B("GH", l, j)
                            tdst, tb = GH[:, l, j, :], B("GH", l, j)
                        else:
                            hsrc, hb = STFT[:, j, :].rearrange("p (s j) -> p s j", j=2), B("STFT", j)
                            tdst, tb = FSS[:, j, :].rearrange("p (s j) -> p s j", j=2), B("STFT", j)
                        S.op("pool", lambda e, halo_v=halo_v, hsrc=hsrc: e.tensor_copy(out=halo_v, in_=hsrc), reads=[hb], writes=[GXb], n=16)
                        S.op("act", lambda e, data=data, psg=psg, n=n, kind=kind: e.copy(out=data, in_=v3(psg[:, 0:n], kind)), reads=[psgb], writes=[GXb])
                        S.op("pool", lambda e, tail_v=tail_v, tdst=tdst: e.tensor_copy(out=tdst, in_=tail_v), reads=[GXb], writes=[tb], n=16)
                        A0, A0b = Fr.next()
                        a0 = v3(A0[:, 0:n], kind)
                        S.op("act", lambda e, a0=a0, psg=psg, n=n, kind=kind, j=j: e.activation(
                            out=a0, in_=v3(psg[:, 0:n], kind), func=AF.Identity, bias=prow(R_FCB + l * NJ + j), scale=prow(R_FCW + (l * 3 + 2) * NJ + j)),
                            reads=[psgb, BPT], writes=[A0b])
                        for k in (1, 0):
                            S.op("dve", lambda e, a0=a0, taps=taps, j=j, k=k: e.scalar_tensor_tensor(
                                out=a0, in0=taps[k], scalar=prow(R_FCW + (l * 3 + k) * NJ + j), in1=a0, op0=ALU.mult, op1=ALU.add),
                                reads=[GXb, A0b, BPT], writes=[A0b])
                        S.op("act", lambda e, A0=A0, n=n: e.activation(out=A0[:, 0:n], in_=A0[:, 0:n], func=AF.Silu), reads=[A0b], writes=[A0b])
                        S.op("dve", lambda e, A0=A0, psu=psu, jj=jj, t0=t0, n=n: e.tensor_tensor(
                            out=Y[:, jj, t0:t0 + n], in0=A0[:, 0:n], in1=psu[:, 0:n], op=ALU.mult), reads=[A0b, psub], writes=[B("Y", jj, b)])
                if half == 0:
                    groups = [[oc] for oc in range(8)]
                elif l == 0:
                    groups = [list(range(8))]
                else:
                    groups = [[0, 1, 2, 3], [4, 5, 6, 7]]
                order = ((g_, oc, blk) for g_ in groups for blk in blocks for oc in g_)
                cur_g, wts_g = None, None
                for g_, oc, (b, t0, n, kind) in order:
                    if g_ is not cur_g:
                        cur_g, wts_g = g_, dict(zip(g_, W.get_n(len(g_))))
                    w, wb = wts_g[oc]
                    ps, pb = PS.next()
                    mm_fm(ps, pb, w, wb, Y, [B("Y", c, b) for c in range(11)], t0, n, 11)
                    S.op("dve", lambda e, ps=ps, oc=oc, t0=t0, n=n: e.tensor_tensor(
                        out=X[:, oc, t0:t0 + n], in0=ps[:, 0:n], in1=X[:, oc, t0:t0 + n], op=ALU.add), reads=[pb, B("X", oc, b)], writes=[B("X", oc, b)])
            if last_sb:
                fm_to_rows(lambda j: GH[:, l, j, :], lambda j: [B("GH", l, j)], NJ, 2, ffn_p[l])
                fm_to_rows(lambda j: FSS[:, j, :], lambda j: [B("STFT", j)], NJ, 32, ffn_s[l])

        def pool_layer(blocks, first_sb, last_sb):
            for (b, t0, n, kind) in blocks:
                R, Rb = norm_R(b, t0, n)
                for g in range(4):
                    w = POOL_W[g]
                    Ds = []
                    for cc in range(2):
                        c = 2 * g + cc
                        HF, HFb = Fr.next()
                        data, _, halo_v, tail_v = ext_views(HF, n, kind, 15)
                        if kind == "p":
                            hsrc, hb = PH[:, c, :], B("PH", c)
                            tdst, tb = PH[:, c, :], B("PH", c)
                        else:
                            hsrc, hb = STPT[:, c, :].rearrange("p (s j) -> p s j", j=15), B("STPT", c)
                            tdst, tb = PSS[:, c, :].rearrange("p (s j) -> p s j", j=15), B("STPT", c)
                        S.op("pool", lambda e, halo_v=halo_v, hsrc=hsrc: e.tensor_copy(out=halo_v, in_=hsrc), reads=[hb], writes=[HFb], n=32)
                        S.op("dve", lambda e, data=data, c=c, R=R, t0=t0, n=n, kind=kind: e.scalar_tensor_tensor(
                            out=data, in0=v3(X[:, c, t0:t0 + n], kind), scalar=prow(R_NMIX + 8 + c), in1=v3(R[:, 0:n], kind), op0=ALU.mult, op1=ALU.mult),
                            reads=[B("X", c, b), Rb, BPT], writes=[HFb])
                        S.op("pool", lambda e, tail_v=tail_v, tdst=tdst: e.tensor_copy(out=tdst, in_=tail_v), reads=[HFb], writes=[tb], n=32)
                        if kind == "p":
                            L = 15 + n

                            def view(buf, lo, hi):
                                return buf[:, lo:hi]
                        else:
                            L = 23

                            def view(buf, lo, hi):
                                return buf[:, 0:16 * 23].rearrange("p (s l) -> p s l", l=23)[:, :, lo:hi]
                        fold = (kind == "p") and not (first_sb and b == 0)
                        cur, curb = HF, HFb
                        for step in range(g if fold else g + 1):
                            sh = 1 << step
                            lo = (1 << (step + 1)) - 1
                            nxt, nxtb = Fr.next()
                            eng = "dve"
                            S.op(eng, lambda e, nxt=nxt, cur=cur, lo=lo, sh=sh, L=L, view=view: e.tensor_tensor(
                                out=view(nxt, lo, L), in0=view(cur, lo, L), in1=view(cur, lo - sh, L - sh), op=ALU.add), reads=[curb], writes=[nxtb])
                            cur, curb = nxt, nxtb
                        if fold:
                            hw = w // 2
                            Tb, Tbb = Bh.next()
                            S.op("act", lambda e, Tb=Tb, cur=cur, n=n: e.copy(out=Tb[:, 0:n], in_=cur[:, 15:15 + n]), reads=[curb], writes=[Tbb], n=n)
                            Th, Thb = ThR.next()
                            S.op("act", lambda e, Th=Th, cur=cur, hw=hw: e.copy(out=Th[:, 0:hw], in_=cur[:, 15 - hw:15]), reads=[curb], writes=[Thb], n=8)
                            Hn, Hnb = Bh.next()
                            S.op("act", lambda e, Hn=Hn, HF=HF, n=n, w=w: e.activation(out=Hn[:, 0:n], in_=HF[:, 15:15 + n], func=AF.Identity, scale=-float(w)),
                                 reads=[HFb], writes=[Hnb], n=n)
                            Ds.append((Tb, Tbb, Th, Thb, Hn, Hnb))
                            continue
                        Dt, Dtb = Bh.next()
                        S.op("dve", lambda e, Dt=Dt, cur=cur, HF=HF, n=n, L=L, w=w, view=view, kind=kind: e.scalar_tensor_tensor(
                            out=v3(Dt[:, 0:n], kind), in0=view(HF, 15, L), scalar=-float(w), in1=view(cur, 15, L), op0=ALU.mult, op1=ALU.add),
                            reads=[curb, HFb], writes=[Dtb])
                        if first_sb and b == 0:
                            tmp, tmpb = Fr.next()

                            S.op("dve", lambda e, tmp=tmp, cur=cur, g=g: e.tensor_tensor(out=tmp[:, 0:15], in0=cur[:, 15:30], in1=rcnt[:, g, 0:15], op=ALU.mult),
                                 reads=[curb, BC], writes=[tmpb])
                            S.op("dve", lambda e, tmp=tmp, Dt=Dt, HF=HF, w=w: e.scalar_tensor_tensor(
                                out=Dt[:, 0:15], in0=HF[:, 15:30], scalar=-float(w), in1=tmp[:, 0:15], op0=ALU.mult, op1=ALU.add),
                                reads=[tmpb, HFb, Dtb], writes=[Dtb])
                        Ds.append((Dt, Dtb))
                    for ec in range(2):
                        ps, pb = PS.next()
                        if len(Ds[0]) == 2:
                            def fp(e, ps=ps, Ds=Ds, g=g, ec=ec, n=n):
                                e.matmul(ps[:, 0:n], lhsT=WP[:, g, 0, ec * 128:(ec + 1) * 128], rhs=Ds[0][0][:, 0:n], start=True, stop=False)
                                return e.matmul(ps[:, 0:n], lhsT=WP[:, g, 1, ec * 128:(ec + 1) * 128], rhs=Ds[1][0][:, 0:n], start=False, stop=True)
                            S.op("pe", fp, reads=[Ds[0][1], Ds[1][1], B("WP")], writes=[pb], cost=2 * (0.51 * n + 22))
                        else:
                            hw = w // 2

                            def fp(e, ps=ps, Ds=Ds, g=g, ec=ec, n=n, hw=hw):
                                ins = None
                                for cc in range(2):
                                    Tb, _, Th, _, Hn, _ = Ds[cc]
                                    wt = WP[:, g, cc, ec * 128:(ec + 1) * 128]
                                    e.matmul(ps[:, 0:n], lhsT=wt, rhs=Tb[:, 0:n], start=(cc == 0), stop=False)
                                    e.matmul(ps[:, hw:n], lhsT=wt, rhs=Tb[:, 0:n - hw], start=False, stop=False, skip_group_check=True)
                                    e.matmul(ps[:, 0:hw], lhsT=wt, rhs=Th[:, 0:hw], start=False, stop=False, skip_group_check=True)
                                    ins = e.matmul(ps[:, 0:n], lhsT=wt, rhs=Hn[:, 0:n], start=False, stop=(cc == 1))
                                return ins
                            rd = [B("WP")]
                            for cc in range(2):
                                rd += [Ds[cc][1], Ds[cc][3], Ds[cc][5]]
                            S.op("pe", fp, reads=rd, writes=[pb], cost=2 * (3 * (0.51 * n + 22) + 70))
                        c = 2 * g + ec
                        S.op("dve", lambda e, ps=ps, c=c, t0=t0, n=n: e.scalar_tensor_tensor(
                            out=X[:, c, t0:t0 + n], in0=ps[:, 0:n], scalar=prow(R_PSC + c), in1=X[:, c, t0:t0 + n], op0=ALU.mult, op1=ALU.add),
                            reads=[pb, B("X", c, b), BPT], writes=[B("X", c, b)])
            if last_sb:
                fm_to_rows(lambda c: PH[:, c, :], lambda c: [B("PH", c)], 8, 15, pool_p)
                for h in range(2):
                    fm_to_rows(lambda c, h=h: PSS[:, c, h * 120:(h + 1) * 120], lambda c: [B("STPT", c)], 8, 120, pool_s[h * 120:(h + 1) * 120, :])

        def final(blocks, dst_of_tile):
            for (b, t0, n, kind) in blocks:
                R, Rb = norm_R(b, t0, n)
                OF = []
                for c in range(8):
                    o, ob = Fr.next()
                    S.op("dve", lambda e, o=o, c=c, R=R, t0=t0, n=n: e.scalar_tensor_tensor(
                        out=o[:, 0:n], in0=X[:, c, t0:t0 + n], scalar=prow(R_NFIN + c), in1=R[:, 0:n], op0=ALU.mult, op1=ALU.mult),
                        reads=[B("X", c, b), Rb, BPT], writes=[ob])
                    OF.append((o, ob))
                for tt in range(n // 128):
                    fm_to_rows(lambda c, tt=tt, OF=OF: OF[c][0][:, tt * 128:(tt + 1) * 128], lambda c, OF=OF: [OF[c][1]], 8, 128, dst_of_tile(b, tt))

        setup()
        for sbi in range(2):
            first_sb, last_sb = (sbi == 0), (sbi == 1)
            blocks = [(0, 0, 512, "p"), (1, 512, 512, "p")]
            tiles = [(xp[sbi * 1024 + tt * 128: sbi * 1024 + (tt + 1) * 128, :], tt * 128, tt // 4) for tt in range(8)]
            if last_sb:
                blocks.append((2, 1024, 128, "s"))
                tiles.append((xs, 1024, 2))
            load_x(tiles)
            if first_sb:
                late_setup()
            norm_to_H(blocks, R_NMIX)
            specs = []
            for c in range(4):
                specs += [wsrc(w_in, C_AC + c * 128), wsrc(w_in, C_AV + c * 128), wsrc(w_in, C_AB + c * 128)]
                hd = c
                specs += [wsrc(w_in, C_BQ + hd * 128), wsrc(w_in, C_BF + hd * 128), wsrc(w_in, C_BG + hd * 128)]
            for oc in range(8):
                specs.append(wsrc(w_out, oc * 128))
            W = WStream(specs)
            hgrn_prologue(blocks)
            has_s = any(k == "s" for (_, _, _, k) in blocks)
            for i in range(4):
                mixer_a(blocks, W, [i])
                hgrn_heads(blocks, W, last_sb, [i], has_s)
            if last_sb:
                fm_to_rows(lambda c: UH[:, c, :], lambda c: [B("UH", c)], 4, 2, conv_p)
                fm_to_rows(lambda c: CSS[:, c, :], lambda c: [B("STCT", c)], 4, 32, conv_s)
            if first_sb:
                for g in range(4):
                    S.dma("pool", WP[:, g, :, :], w_pool[g].rearrange("(k p) n -> p k n", p=128), writes=[B("WP")])
                for g in range(4):
                    S.op("pool", lambda e, g=g: e.tensor_scalar(out=WP[:, g, :, :], in0=WP[:, g, :, :], scalar1=1.0 / POOL_W[g], scalar2=0.0,
                                                                op0=ALU.mult, op1=ALU.add), reads=[B("WP")], writes=[B("WP")], n=512)
            w_out_phase(blocks, W)
            ffn(0, blocks, last_sb)
            pool_layer(blocks, first_sb, last_sb)
            ffn(1, blocks, last_sb)

            def dst_of_tile(b, tt, sbi=sbi):
                if b == 2:
                    return y_s
                r0 = sbi * 1024 + b * 512 + tt * 128
                return y_p[r0:r0 + 128, :]
            final(blocks, dst_of_tile)
        S.finish()
        S.emit(nc, esem, dsem)
    return nc, S


_CACHE = {}


def kernel(x_prompt, x_sample, state_conv_a, state_hgrn, state_pool, state_ffn, norm_mix, norm_ffn,
           norm_final, w_in, conv_a_w, hgrn_lower_bounds, hgrn_norm, w_out, pool_w, pool_scale,
           ffn_w_gu, ffn_conv_w, ffn_conv_b, ffn_w_down):
    f = lambda a: np.ascontiguousarray(np.asarray(a, dtype=np.float32))
    ncores = 8
    rows = [f(norm_mix).reshape(16, 128), f(norm_ffn).reshape(16, 128), f(norm_final).reshape(8, 128),
            f(conv_a_w).reshape(12, 128), f(hgrn_lower_bounds).reshape(12, 128), f(hgrn_norm).reshape(4, 128),
            f(pool_scale).reshape(8, 128), f(ffn_conv_w).reshape(2 * 3 * NJ, 128), f(ffn_conv_b).reshape(2 * NJ, 128)]
    prm = np.concatenate(rows + [np.zeros((256 - 252, 128), np.float32)], axis=0)
    assert prm.shape == (256, 128)
    shared = {"prm": prm, "w_in": f(w_in)[0], "w_out": f(w_out)[0], "w_pool": f(pool_w)[0],
              "w_gu": f(ffn_w_gu), "w_dn": f(ffn_w_down)}
    xpf, xsf = f(x_prompt), f(x_sample)
    sc, sh, sp_, sf = f(state_conv_a), f(state_hgrn), f(state_pool), f(state_ffn)
    in_maps = []
    for i in range(ncores):
        sl = slice(16 * i, 16 * i + 16)
        m = dict(shared)
        m["xp"] = xpf[i]
        m["xs"] = np.ascontiguousarray(xsf[sl].reshape(128, D))
        m["stc"] = np.ascontiguousarray(sc[0, sl].reshape(32, 512))
        m["sth"] = np.ascontiguousarray(sh[0, sl])
        m["stp"] = np.ascontiguousarray(sp_[0, sl].reshape(240, D))
        m["stf"] = np.ascontiguousarray(sf[:, sl].reshape(2, 32, DFF))
        in_maps.append(m)
    if "nc" not in _CACHE:
        _CACHE["nc"] = build_nc()[0]
    nc = _CACHE["nc"]
    res = run_bass_kernel_spmd(nc, in_maps, core_ids=list(range(ncores)))
    R = res.results
    if DEBUG:
        _CACHE["dbg"] = np.asarray(R[0]["dbg"])
    cat = lambda k, ax=0: np.concatenate([np.asarray(r[k], dtype=np.float32) for r in R], axis=ax)
    y_prompt = np.stack([np.asarray(r["y_p"], np.float32) for r in R], 0)
    y_sample = cat("y_s").reshape(128, 8, D)
    conv_p = np.stack([np.asarray(r["conv_p"], np.float32) for r in R], 0)[None]
    conv_s = cat("conv_s").reshape(1, 128, 2, 512)
    hg_p = np.stack([np.asarray(r["hg_p"], np.float32) for r in R], 0)[None]
    hg_s = cat("hg_s")[None]
    pool_p = np.stack([np.asarray(r["pool_p"], np.float32) for r in R], 0)[None]
    pool_s = cat("pool_s").reshape(1, 128, 15, D)
    ffn_p = np.stack([np.asarray(r["ffn_p"], np.float32) for r in R], 1)
    ffn_s = cat("ffn_s", 1).reshape(2, 128, 2, DFF)
    return (y_prompt, y_sample, conv_p, conv_s, hg_p, hg_s, pool_p, pool_s, ffn_p, ffn_s)
```

```python
import contextlib
import numpy as np
import concourse.bass as bass
import concourse.mybir as mybir
from concourse.bass_utils import run_bass_kernel_spmd

F32 = mybir.dt.float32
BF16 = mybir.dt.bfloat16
AF = mybir.ActivationFunctionType
ALU = mybir.AluOpType

SAME_ENGINE_SYNC = True
EMBED_WAITS = True
DEBUG = False
DBG_MAP = {}


class Buf:
    __slots__ = ("name", "w", "r")

    def __init__(self, name):
        self.name = name
        self.w = None
        self.r = []


class Op:
    __slots__ = ("id", "eng", "kind", "fn", "args", "preds", "dur", "lat", "prio", "start", "finish", "idx", "tok", "succs")


def _cost(eng, n):
    if eng == "act":
        return 220 + 0.65 * n
    if eng == "dve":
        return 110 + 1.0 * n
    if eng == "pool":
        return 160 + 1.6 * n
    return 60


class Sched:
    ENGS = ("pe", "act", "dve", "pool", "sp")
    LIST_SCHEDULE = True
    AGE_WEIGHT = 10.0

    def __init__(self, n_dma_slots):
        self.ops = []
        self.nslots = n_dma_slots
        self.bufs = {}
        self.cnt = {e: 0 for e in self.ENGS}

    def B(self, *key):
        b = self.bufs.get(key)
        if b is None:
            b = Buf(key)
            self.bufs[key] = b
        return b

    def _new(self, eng, kind, reads, writes, dur, lat):
        o = Op()
        o.id = len(self.ops)
        o.eng = eng
        o.kind = kind
        o.dur = dur
        o.lat = lat
        preds = {}
        RANK = {"war": 0, "waw": 1, "raw": 2}

        def add(pid, k):
            if pid is None:
                return
            if pid not in preds or RANK[k] > RANK[preds[pid]]:
                preds[pid] = k
        for b in reads:
            add(b.w, "raw")
        for b in writes:
            add(b.w, "waw")
            for rid in b.r:
                if rid != o.id:
                    add(rid, "war")
        o.preds = preds
        for b in reads:
            b.r.append(o.id)
        for b in writes:
            b.w = o.id
            b.r = []
        self.ops.append(o)
        self.cnt[eng] += 1
        return o

    def op(self, eng, fn, reads=(), writes=(), n=512, cost=None):
        dur = cost if cost is not None else _cost(eng, n)
        o = self._new(eng, "op", reads, writes, dur, dur)
        o.fn = fn

    MAX_SWDGE_INFLIGHT = 4

    def dma(self, q, out_ap, in_ap, reads=(), writes=(), nbytes=262144, **kw):
        issue = 1200 if q == "pool" else 120
        o = self._new(q, "dma", reads, writes, issue, issue + 2500 + nbytes / 150.0)
        o.args = (out_ap, in_ap, kw)
        if q == "pool":
            hist = self.__dict__.setdefault("_swdge", [])
            if len(hist) >= self.MAX_SWDGE_INFLIGHT:
                o.preds.setdefault(hist[-self.MAX_SWDGE_INFLIGHT], "raw")
            hist.append(o.id)

    def finish(self, q="sp"):
        self.final_q = q

    def schedule(self):
        ops = self.ops
        N = len(ops)
        for o in ops:
            o.succs = []
        for o in ops:
            for p in o.preds:
                ops[p].succs.append(o.id)
        for o in reversed(ops):
            m = 0.0
            for sid in o.succs:
                if ops[sid].prio > m:
                    m = ops[sid].prio
            o.prio = o.lat + m
        if not self.LIST_SCHEDULE:
            t = 0.0
            for o in ops:
                o.start = t
                t += 1.0
                o.finish = t
            return
        for o in ops:
            o.prio = o.prio - self.AGE_WEIGHT * o.id
        npred = [len(o.preds) for o in ops]
        ready_t = [0.0] * N
        avail = {e: [] for e in self.ENGS}
        for o in ops:
            if npred[o.id] == 0:
                avail[o.eng].append(o.id)
        free = {e: 0.0 for e in self.ENGS}
        done = 0
        XLAT = 300.0
        while done < N:
            best = None
            for e in self.ENGS:
                av = avail[e]
                if not av:
                    continue
                f = free[e]
                mn = min(max(f, ready_t[i]) for i in av)
                pick = None
                for i in av:
                    st = max(f, ready_t[i])
                    if st <= mn + 200.0:
                        if pick is None or ops[i].prio > ops[pick].prio:
                            pick = i
                st = max(f, ready_t[pick])
                if best is None or st < best[0]:
                    best = (st, e, pick)
            st, e, i = best
            o = ops[i]
            avail[e].remove(i)
            o.start = st
            free[e] = st + o.dur
            o.finish = st + o.lat
            done += 1
            for sid in o.succs:
                so = ops[sid]
                t = o.finish + (XLAT if so.eng != e else 40.0)
                if t > ready_t[sid]:
                    ready_t[sid] = t
                npred[sid] -= 1
                if npred[sid] == 0:
                    avail[so.eng].append(sid)
        self.makespan = max(o.finish for o in ops)

    def emit(self, nc, esem, dsem):
        self.schedule()
        ops = self.ops
        order = sorted(ops, key=lambda o: (o.start, o.id))
        cnt = {e: 0 for e in self.ENGS}
        slot_val = [0] * self.nslots
        slot_prev = {}
        nxt = 0
        for o in order:
            if o.kind == "op":
                cnt[o.eng] += 1
                o.tok = (("e", o.eng), cnt[o.eng])
            else:
                k = nxt
                nxt = (nxt + 1) % self.nslots
                if slot_val[k] > 0:
                    slot_prev[o.id] = (("d", k), slot_val[k])
                slot_val[k] += 16
                o.tok = (("d", k), slot_val[k])
        prog = {e: [] for e in self.ENGS}
        seen = {e: {} for e in self.ENGS}
        for o in order:
            need = {}
            sn = seen[o.eng]

            def want(key, val):
                if sn.get(key, 0) >= val:
                    return
                if need.get(key, 0) < val:
                    need[key] = val
            for pid, kind in o.preds.items():
                p = ops[pid]
                if p.kind == "op" and p.eng == o.eng:
                    if o.eng == "pe" or kind == "war" or not SAME_ENGINE_SYNC:
                        continue
                want(p.tok[0], p.tok[1])
            if o.id in slot_prev:
                want(*slot_prev[o.id])
            for k_, v_ in need.items():
                sn[k_] = v_
            prog[o.eng].append((o, need))
        fin = {}
        fq = getattr(self, "final_q", "sp")
        for k in range(self.nslots):
            if slot_val[k] > 0 and seen[fq].get(("d", k), 0) < slot_val[k]:
                fin[("d", k)] = slot_val[k]
        self.prog = prog

        def semof(key):
            return esem[key[1]] if key[0] == "e" else dsem[key[1]]

        class _First:
            def __init__(self, e):
                self._e = e
                self.first = None

            def __getattr__(self, name):
                f = getattr(self._e, name)

                def g(*a, **k):
                    r = f(*a, **k)
                    if self.first is None:
                        self.first = r
                    return r
                return g

        def run(engname, eng):
            embed = EMBED_WAITS and engname != "pe"
            for o, need in prog[engname]:
                items = list(need.items())
                tail = items.pop() if (embed and items) else None
                for key, val in items:
                    eng.wait_ge(semof(key), val)
                if o.kind == "op":
                    px = _First(eng)
                    ins = o.fn(px)
                    if tail is not None:
                        px.first._wait_ge(semof(tail[0]), tail[1])
                    ins.then_inc(esem[engname], 1)
                else:
                    out_ap, in_ap, kw = o.args
                    ins = eng.dma_start(out=out_ap, in_=in_ap, **kw)
                    if tail is not None:
                        ins._wait_ge(semof(tail[0]), tail[1])
                    ins.then_inc(dsem[o.tok[0][1]], 16)
            if engname == fq:
                for key, val in fin.items():
                    eng.wait_ge(semof(key), val)

        with nc.Block() as block:
            @block.tensor
            def _(e):
                run("pe", e)

            @block.scalar
            def _(e):
                run("act", e)

            @block.vector
            def _(e):
                run("dve", e)

            @block.gpsimd
            def _(e):
                run("pool", e)

            @block.sync
            def _(e):
                run("sp", e)


class Ring:
    def __init__(self, S, name, tensors, bufs=None):
        self.t = tensors
        self.b = bufs if bufs is not None else [S.B(name, i) for i in range(len(tensors))]
        self.i = 0

    def next(self):
        k = self.i
        self.i = (k + 1) % len(self.t)
        return self.t[k], self.b[k]


D = 1024
NTL = 1152
DFF = 2816
NJ = 22
EPS = 1e-6
POOL_W = (2, 4, 8, 16)
R_NMIX, R_NFFN, R_NFIN, R_CAW, R_LB, R_HGN, R_PSC, R_FCW, R_FCB = 0, 16, 32, 40, 52, 64, 68, 76, 208
C_AC, C_AB, C_AV, C_BQ, C_BF, C_BI, C_BG = 0, 512, 1024, 1536, 2048, 2560, 3072

NDMA = 24


def build_nc(stop_after=None):
    nc = bass.Bass("TRN2", target_bir_lowering=False)

    def din(name, shape):
        return nc.dram_tensor(name, shape, F32, kind="ExternalInput").ap()

    def dout(name, shape):
        return nc.dram_tensor(name, shape, F32, kind="ExternalOutput").ap()

    xp = din("xp", [2048, D]); xs = din("xs", [128, D])
    stc = din("stc", [32, 512]); sth = din("sth", [16, 4, 128, 128])
    stp = din("stp", [240, D]); stf = din("stf", [2, 32, DFF]); prm = din("prm", [256, 128])
    w_in = din("w_in", [D, 3584]); w_out = din("w_out", [D, D]); w_pool = din("w_pool", [4, 256, 256])
    w_gu = din("w_gu", [2, D, 2 * DFF]); w_dn = din("w_dn", [2, DFF, D])
    y_p = dout("y_p", [2048, D]); y_s = dout("y_s", [128, D])
    conv_p = dout("conv_p", [2, 512]); conv_s = dout("conv_s", [32, 512])
    hg_p = dout("hg_p", [4, 128, 128]); hg_s = dout("hg_s", [16, 4, 128, 128])
    pool_p = dout("pool_p", [15, D]); pool_s = dout("pool_s", [240, D])
    ffn_p = dout("ffn_p", [2, 2, DFF]); ffn_s = dout("ffn_s", [2, 32, DFF])

    S = Sched(NDMA)
    B = S.B
    dbg_d = dout("dbg", [128, 16384]) if DEBUG else None
    dbg_off = [0]

    def dump(name, ap, bufs, n):
        if not DEBUG or name in DBG_MAP:
            return
        off = dbg_off[0]
        dbg_off[0] += n
        DBG_MAP[name] = (off, n)
        S.dma("sp", dbg_d[:, off:off + n], ap, reads=bufs)
    with contextlib.ExitStack() as es:
        _n = [0]

        def sb(shape, dt, name=None):
            _n[0] += 1
            return es.enter_context(nc.sbuf_tensor(name or ("t%d" % _n[0]), shape, dt))

        X = sb([128, 8, NTL], F32, "X")
        H = sb([128, 8, NTL], BF16, "H")
        Y = sb([128, 11, NTL], BF16, "Y")
        WB = Ring(S, "WB", [sb([128, 1408], BF16) for _ in range(8)])
        Fr = Ring(S, "F", [sb([128, 528], F32) for _ in range(13)])
        GXr = Ring(S, "GX", [sb([128, 516], F32) for _ in range(2)])
        Rr = Ring(S, "Rr", [sb([128, 512], F32) for _ in range(2)])
        EGr = Ring(S, "EGr", [sb([128, 16], F32) for _ in range(6)])
        SbR = Ring(S, "SbR", [sb([128, 128], BF16) for _ in range(6)])
        Bh = Ring(S, "Bh", [sb([128, 512], BF16) for _ in range(12)])
        ThR = Ring(S, "ThR", [sb([128, 8], BF16) for _ in range(4)])
        Vall = sb([128, 9, 512], BF16, "Vall")
        XIN = Ring(S, "XIN", [sb([128, 1024], F32) for _ in range(2)])
        _banks = [es.enter_context(nc.psum_tensor("ps%d" % i, [128, 512], F32)) for i in range(8)]
        _bb = [S.B("bank", i) for i in range(8)]
        PS = Ring(S, "PS", _banks, _bb)
        PSg = Ring(S, "PSg", _banks[0:4], _bb[0:4])
        PSO = Ring(S, "PSO", _banks[4:6], _bb[4:6])
        PSC = Ring(S, "PSC", _banks[6:8], _bb[6:8])
        ident = sb([128, 128], F32, "ident")
        onesb = sb([128, 128], BF16, "onesb")
        cmask_p = sb([128, 128], F32, "cmask_p")
        cmask_s = sb([128, 128], F32, "cmask_s")
        seqm = sb([128, 16], F32, "seqm")
        scanm_p = sb([128, 512], F32, "scanm_p")
        scanm_s = sb([128, 128], F32, "scanm_s")
        rcnt = sb([128, 4, 16], F32, "rcnt")
        PT = sb([128, 256], F32, "PT")
        LB = sb([128, 16], F32, "LB")
        epsT = sb([128, 1], F32, "epsT")
        S32 = sb([128, 4, 128], F32, "S32")
        Sb16 = sb([128, 4, 128], BF16, "Sb16")
        S0 = sb([128, 16, 128], F32, "S0")
        S0b = sb([128, 16, 128], BF16, "S0b")
        WI = S0.bitcast(BF16)[:].rearrange("p a b -> p (a b)").rearrange("p (k n) -> p k n", n=512)
        Vblk = sb([128, 16, 128], BF16, "Vblk")
        STCT = sb([128, 4, 32], F32, "STCT")
        STFT = sb([128, NJ, 32], F32, "STFT")
        STPT = sb([128, 8, 240], F32, "STPT")
        UH = sb([128, 4, 2], F32, "UH")
        GH = sb([128, 2, NJ, 2], F32, "GH")
        PH = sb([128, 8, 15], F32, "PH")
        CSS, FSS, PSS = STCT, STFT, STPT
        WP = sb([128, 4, 2, 256], BF16, "WP")

        esem = {e: es.enter_context(nc.semaphore("sem_" + e)) for e in Sched.ENGS}
        dsem = [es.enter_context(nc.semaphore("dsem%d" % i)) for i in range(NDMA)]

        def prow(r):
            return PT[:, r:r + 1]

        BPT = B("PT")
        BID = B("ident")

        def fm_to_rows(src_fn, src_bufs_fn, nch, r, dst):
            for j0 in range(0, nch, 8):
                jn = min(8, nch - j0)
                st, stb = XIN.next()
                for h0 in range(0, jn, 4):
                    hn = min(4, jn - h0)
                    ps, pb = PS.next()

                    def tr(e, ps=ps, j0=j0, h0=h0, hn=hn):
                        ins = None
                        for k in range(hn):
                            ins = e.transpose(out=ps[0:r, k * 128:(k + 1) * 128], in_=src_fn(j0 + h0 + k), identity=ident[:])
                        return ins
                    rd = [BID]
                    for k in range(hn):
                        rd += src_bufs_fn(j0 + h0 + k)
                    S.op("pe", tr, reads=rd, writes=[pb], cost=110 * hn)
                    S.op("act", lambda e, ps=ps, st=st, h0=h0, hn=hn: e.copy(out=st[0:r, h0 * 128:(h0 + hn) * 128], in_=ps[0:r, 0:hn * 128]),
                         reads=[pb], writes=[stb], n=hn * 128)
                S.dma("sp", dst[:, j0 * 128:(j0 + jn) * 128], st[0:r, 0:jn * 128], reads=[stb])

        def rows_to_fm(src, r, nch, dst_fn, dst_bufs_fn):
            for j0 in range(0, nch, 8):
                jn = min(8, nch - j0)
                st, stb = XIN.next()
                S.dma("sp", st[0:r, 0:jn * 128], src[:, j0 * 128:(j0 + jn) * 128], writes=[stb])
                per = max(1, 512 // r)
                k = 0
                while k < jn:
                    kn = min(per, jn - k)
                    ps, pb = PS.next()

                    def tr(e, ps=ps, st=st, k=k, kn=kn):
                        ins = None
                        for q in range(kn):
                            ins = e.transpose(out=ps[:, q * r:(q + 1) * r], in_=st[0:r, (k + q) * 128:(k + q + 1) * 128], identity=ident[0:r, 0:r])
                        return ins
                    S.op("pe", tr, reads=[stb, BID], writes=[pb], cost=110 * kn)
                    for q in range(kn):
                        jq = j0 + k + q
                        S.op("act", lambda e, ps=ps, q=q, jq=jq: e.copy(out=dst_fn(jq), in_=ps[:, q * r:(q + 1) * r]),
                             reads=[pb], writes=dst_bufs_fn(jq), n=r)
                    k += kn

        class WStream:
            def __init__(self, specs, depth=3):
                self.specs = specs
                self.i = 0
                self.depth = depth
                self.ready = []
                self._fill(depth)

            def _fill(self, k):
                while len(self.ready) < k and self.i < len(self.specs):
                    src, kc, ncols = self.specs[self.i]
                    self.i += 1
                    wb, wbb = WB.next()
                    n = kc * ncols
                    view = wb[:, 0:n].rearrange("p (k n) -> p k n", n=ncols)
                    S.dma("pool", view, src, writes=[wbb])
                    self.ready.append((view, wbb))

            def get_n(self, n):
                assert n + (self.depth if self.i + max(0, n - len(self.ready)) < len(self.specs) else 0) <= len(WB.t)
                self._fill(n)
                out = [self.ready.pop(0) for _ in range(n)]
                self._fill(self.depth)
                return out

            def get(self):
                return self.get_n(1)[0]

        def wsrc(w2d, col0, ncols=128):
            return (w2d[:, col0:col0 + ncols].rearrange("(k p) n -> p k n", p=128), w2d.shape[0] // 128, ncols)

        def mm_fm(ps, pb, w, wbuf, src, src_bufs, t0, n, nk):
            def f(e):
                ins = None
                for k in range(nk):
                    ins = e.matmul(ps[:, 0:n], lhsT=w[:, k, :], rhs=src[:, k, t0:t0 + n], start=(k == 0), stop=(k == nk - 1))
                return ins
            S.op("pe", f, reads=[wbuf] + src_bufs, writes=[pb], cost=nk * (0.51 * n + 22))

        def BX(b):
            return [B("X", c, b) for c in range(8)]

        def BH(b):
            return [B("H", c, b) for c in range(8)]

        def BY(b):
            return [B("Y", b)]

        def setup():
            S.op("pool", lambda e: e.memset(ident[:], 1.0), writes=[BID])
            S.op("pool", lambda e: e.affine_select(out=ident[:], in_=ident[:], pattern=[[-1, 128]], compare_op=ALU.is_equal, fill=0.0,
                                                   base=0, channel_multiplier=1), reads=[BID], writes=[BID])

            def P1(fn, writes, reads=()):
                S.op("pool", fn, reads=list(reads), writes=list(writes))
            P1(lambda e: e.memset(onesb[:], 1.0), [B("c_ones")])
            P1(lambda e: e.memset(epsT[:], EPS), [B("c_eps")])
            P1(lambda e: e.memset(cmask_p[:], 1.0), [B("c_cmp")])
            P1(lambda e: e.affine_select(out=cmask_p[:], in_=cmask_p[:], pattern=[[1, 128]], compare_op=ALU.is_ge, fill=0.0, base=0, channel_multiplier=-1), [B("c_cmp")], [B("c_cmp")])
            P1(lambda e: e.memset(cmask_p[0:64, 64:128], 0.0), [B("c_cmp")], [B("c_cmp")])
            P1(lambda e: e.memset(cmask_s[:], 1.0), [B("c_cms")])
            P1(lambda e: e.affine_select(out=cmask_s[:], in_=cmask_s[:], pattern=[[1, 128]], compare_op=ALU.is_ge, fill=0.0, base=0, channel_multiplier=-1), [B("c_cms")], [B("c_cms")])
            cm3 = cmask_s[:].rearrange("p (n t) -> p n t", t=8)
            P1(lambda e: e.affine_select(out=cm3, in_=cm3, pattern=[[-8, 16], [0, 8]], compare_op=ALU.is_ge, fill=0.0, base=0, channel_multiplier=1), [B("c_cms")], [B("c_cms")])
            P1(lambda e: e.memset(seqm[:], 1.0), [B("c_seqm")])
            P1(lambda e: e.affine_select(out=seqm[:], in_=seqm[:], pattern=[[-8, 16]], compare_op=ALU.is_ge, fill=0.0, base=0, channel_multiplier=1), [B("c_seqm")], [B("c_seqm")])
            P1(lambda e: e.affine_select(out=seqm[:], in_=seqm[:], pattern=[[8, 16]], compare_op=ALU.is_ge, fill=0.0, base=7, channel_multiplier=-1), [B("c_seqm")], [B("c_seqm")])
            P1(lambda e: e.memset(scanm_p[:], 1.0), [B("c_scp")])
            P1(lambda e: e.memset(scanm_p[:].rearrange("p (c t) -> p c t", t=64)[:, :, 0:1], 0.0), [B("c_scp")], [B("c_scp")])
            P1(lambda e: e.memset(scanm_s[:], 1.0), [B("c_scs")])
            P1(lambda e: e.memset(scanm_s[:].rearrange("p (c t) -> p c t", t=8)[:, :, 0:1], 0.0), [B("c_scs")], [B("c_scs")])
            for gi, w in enumerate(POOL_W):
                P1(lambda e, gi=gi, w=w: e.memset(rcnt[:, gi, :], 1.0), [B("c_rcnt")], [B("c_rcnt")])
                for t in range(w - 1):
                    P1(lambda e, gi=gi, t=t, w=w: e.memset(rcnt[:, gi, t:t + 1], float(w) / (t + 1)), [B("c_rcnt")], [B("c_rcnt")])
            P1(lambda e: e.memset(UH[:], 0.0), [B("UH", c) for c in range(4)])
            P1(lambda e: e.memset(GH[:], 0.0), [B("GH", l, j) for l in range(2) for j in range(NJ)])
            P1(lambda e: e.memset(PH[:], 0.0), [B("PH", c) for c in range(8)])
            P1(lambda e: e.memset(S32[:], 0.0), [B("S32", h) for h in range(4)])
            P1(lambda e: e.memset(epsT[:], EPS), [B("consts")], [B("c_ones"), B("c_eps"), B("c_cmp"), B("c_cms"), B("c_seqm"), B("c_scp"), B("c_scs"), B("c_rcnt")])
            S.op("pool", lambda e: e.tensor_copy(out=Sb16[:], in_=S32[:]), reads=[B("S32", h) for h in range(4)], writes=[B("Sb16", h) for h in range(4)])
            st, stb = XIN.next()
            S.dma("sp", st[:, 0:256].rearrange("p (a c) -> p a c", c=128), prm.rearrange("(a p) c -> p a c", p=128), writes=[stb])
            ps, pb = PS.next()

            def trp(e, ps=ps, st=st):
                e.transpose(out=ps[:, 0:128], in_=st[:, 0:128], identity=ident[:])
                return e.transpose(out=ps[:, 128:256], in_=st[:, 128:256], identity=ident[:])
            S.op("pe", trp, reads=[stb, BID], writes=[pb], cost=220)
            S.op("act", lambda e, ps=ps: e.copy(out=PT[:], in_=ps[:, 0:256]), reads=[pb], writes=[BPT], n=256)
            Le, Leb = Fr.next()
            S.op("act", lambda e: e.activation(out=Le[:, 0:12], in_=PT[:, R_LB:R_LB + 12], func=AF.Exp), reads=[BPT], writes=[Leb])

            BLe = Leb
            D1 = lambda fn: S.op("dve", fn, reads=[BLe, B("LB")], writes=[BLe, B("LB")])
            D1(lambda e: e.tensor_tensor(out=Le[:, 12:16], in0=Le[:, 0:4], in1=Le[:, 4:8], op=ALU.add))
            D1(lambda e: e.tensor_tensor(out=Le[:, 12:16], in0=Le[:, 12:16], in1=Le[:, 8:12], op=ALU.add))
            D1(lambda e: e.reciprocal(out=Le[:, 12:16], in_=Le[:, 12:16]))
            D1(lambda e: e.tensor_tensor(out=LB[:, 0:4], in0=Le[:, 0:4], in1=Le[:, 12:16], op=ALU.mult))
            D1(lambda e: e.tensor_scalar(out=LB[:, 4:8], in0=LB[:, 0:4], scalar1=-1.0, scalar2=1.0, op0=ALU.mult, op1=ALU.add))
            D1(lambda e: e.tensor_scalar(out=LB[:, 8:12], in0=LB[:, 0:4], scalar1=-1.0, scalar2=0.0, op0=ALU.add, op1=ALU.add))
            dump("LB", LB[:, 0:16], [B("LB")], 16)
            dump("PT", PT[:, 0:256], [BPT], 256)

        def late_setup():
            rows_to_fm(stc, 32, 4, lambda j: STCT[:, j, :], lambda j: [B("STCT", j)])
            for h in range(2):
                rows_to_fm(stp[h * 120:(h + 1) * 120, :], 120, 8, lambda j, h=h: STPT[:, j, h * 120:(h + 1) * 120], lambda j: [B("STPT", j)])


        BC = B("consts")

        def load_x(tiles):
            for src, t0, b in tiles:
                for half in range(2):
                    st, stb = Fr.next()
                    S.dma("sp", st[:, 0:512], src[:, half * 512:(half + 1) * 512], writes=[stb], nbytes=2 ** 18)
                    ps, pb = PS.next()

                    def tr(e, ps=ps, st=st):
                        ins = None
                        for q in range(4):
                            ins = e.transpose(out=ps[:, q * 128:(q + 1) * 128], in_=st[:, q * 128:(q + 1) * 128], identity=ident[:])
                        return ins
                    S.op("pe", tr, reads=[stb, BID], writes=[pb], cost=440)
                    eng = "act" if half == 0 else "dve"

                    def cp(e, ps=ps, half=half, t0=t0, eng=eng):
                        o = X[:, half * 4:half * 4 + 4, t0:t0 + 128]
                        i = ps[:, :].rearrange("p (q t) -> p q t", t=128)
                        if eng == "act":
                            return e.copy(out=o, in_=i)
                        return e.tensor_copy(out=o, in_=i)
                    S.op(eng, cp, reads=[pb], writes=[B("X", c, b) for c in range(half * 4, half * 4 + 4)])

        def norm_R(b, t0, n, split=False):
            ps, pb = PS.next()
            for c in range(8):
                sq, sqb = Bh.next()
                if not (split and c % 2 == 1):
                    S.op("act", lambda e, sq=sq, c=c: e.activation(out=sq[:, 0:n], in_=X[:, c, t0:t0 + n], func=AF.Square),
                         reads=[B("X", c, b)], writes=[sqb])
                else:
                    S.op("dve", lambda e, sq=sq, c=c: e.tensor_tensor(out=sq[:, 0:n], in0=X[:, c, t0:t0 + n], in1=X[:, c, t0:t0 + n], op=ALU.mult),
                         reads=[B("X", c, b)], writes=[sqb])
                S.op("pe", lambda e, ps=ps, sq=sq, c=c: e.matmul(ps[:, 0:n], lhsT=onesb[:], rhs=sq[:, 0:n], start=(c == 0), stop=(c == 7)),
                     reads=[sqb, BC], writes=[pb], cost=0.51 * n + 22)
            R, Rb = Rr.next()
            S.op("act", lambda e: e.activation(out=R[:, 0:n], in_=ps[:, 0:n], func=AF.Ln, bias=epsT[:, 0:1], scale=1.0 / D),
                 reads=[pb, BC], writes=[Rb])
            S.op("act", lambda e: e.activation(out=R[:, 0:n], in_=R[:, 0:n], func=AF.Exp, scale=-0.5), reads=[Rb], writes=[Rb])
            return R, Rb

        def norm_to_H(blocks, grow):
            for (b, t0, n, kind) in blocks:
                R, Rb = norm_R(b, t0, n, split=True)
                for c in range(8):
                    S.op("dve", lambda e, c=c, R=R, t0=t0, n=n: e.scalar_tensor_tensor(
                        out=H[:, c, t0:t0 + n], in0=X[:, c, t0:t0 + n], scalar=prow(grow + c), in1=R[:, 0:n], op0=ALU.mult, op1=ALU.mult),
                        reads=[B("X", c, b), Rb, BPT], writes=[B("H", c, b)])

        def ext_views(buf, n, kind, halo):
            if kind == "p":
                return (buf[:, halo:halo + n], [buf[:, k:k + n] for k in range(halo + 1)], buf[:, 0:halo], buf[:, n:n + halo])
            L = halo + 8
            v = buf[:, 0:16 * L].rearrange("p (s l) -> p s l", l=L)
            return (v[:, :, halo:L], [v[:, :, k:k + 8] for k in range(halo + 1)], v[:, :, 0:halo], v[:, :, 8:L])

        def v3(ap, kind):
            return ap if kind == "p" else ap.rearrange("p (s l) -> p s l", l=8)

        def mixer_a(blocks, W, cs=range(4)):
            for c in cs:
                (wc, wcb), (wv, wvb), (wbm, wbb) = W.get_n(3)
                for (b, t0, n, kind) in blocks:
                    psc, pscb = PSg.next(); mm_fm(psc, pscb, wc, wcb, H, BH(b), t0, n, 8)
                    psv, psvb = PSg.next(); mm_fm(psv, psvb, wv, wvb, H, BH(b), t0, n, 8)
                    psb, psbb = PSg.next(); mm_fm(psb, psbb, wbm, wbb, H, BH(b), t0, n, 8)
                    T1, T1b = Fr.next()
                    S.op("act", lambda e, T1=T1, psc=psc, n=n: e.copy(out=T1[:, 0:n], in_=psc[:, 0:n]), reads=[pscb], writes=[T1b])
                    U, Ub = GXr.next()
                    data, taps, halo_v, tail_v = ext_views(U, n, kind, 2)
                    if kind == "p":
                        hsrc, hb = UH[:, c, :], B("UH", c)
                        tdst, tb = UH[:, c, :], B("UH", c)
                    else:
                        hsrc, hb = STCT[:, c, :].rearrange("p (s j) -> p s j", j=2), B("STCT", c)
                        tdst, tb = CSS[:, c, :].rearrange("p (s j) -> p s j", j=2), B("STCT", c)
                    S.op("pool", lambda e, halo_v=halo_v, hsrc=hsrc: e.tensor_copy(out=halo_v, in_=hsrc), reads=[hb], writes=[Ub], n=16)
                    S.op("dve", lambda e, data=data, T1=T1, psv=psv, n=n, kind=kind: e.tensor_tensor(
                        out=data, in0=v3(T1[:, 0:n], kind), in1=v3(psv[:, 0:n], kind), op=ALU.mult), reads=[T1b, psvb], writes=[Ub])
                    S.op("pool", lambda e, tail_v=tail_v, tdst=tdst: e.tensor_copy(out=tdst, in_=tail_v), reads=[Ub], writes=[tb], n=16)
                    A0, A0b = Fr.next()
                    a0 = v3(A0[:, 0:n], kind)
                    S.op("act", lambda e, a0=a0, taps=taps, c=c: e.activation(
                        out=a0, in_=taps[2], func=AF.Identity, scale=prow(R_CAW + 8 + c)), reads=[Ub, BPT], writes=[A0b])
                    for k in (1, 0):
                        S.op("dve", lambda e, a0=a0, taps=taps, c=c, k=k: e.scalar_tensor_tensor(
                            out=a0, in0=taps[k], scalar=prow(R_CAW + 4 * k + c), in1=a0, op0=ALU.mult, op1=ALU.add), reads=[Ub, A0b, BPT], writes=[A0b])
                    S.op("dve", lambda e, A0=A0, psb=psb, c=c, t0=t0, n=n: e.tensor_tensor(
                        out=Y[:, c, t0:t0 + n], in0=A0[:, 0:n], in1=psb[:, 0:n], op=ALU.mult), reads=[A0b, psbb], writes=[B("Y", c, b)])

        def hgrn(blocks, W, last_sb, hds=range(4), prologue=True):
            has_s = any(k == "s" for (_, _, _, k) in blocks)
            if prologue:
                hgrn_prologue(blocks)
            hgrn_heads(blocks, W, last_sb, hds, has_s)

        def hgrn_prologue(blocks):
            for k2 in range(4):
                S.dma("pool", WI[:, 2 * k2:2 * k2 + 2, :], w_in[256 * k2:256 * k2 + 256, C_BI:C_BI + 512].rearrange("(k p) n -> p k n", p=128),
                      reads=[B("S0")] if k2 else [], writes=[B("S0")], nbytes=2 ** 19)
            for (b, t0, n, kind) in blocks:
                for tt in range(n // 128):
                    gt = (t0 + tt * 128) // 128
                    psv, psvb = PSg.next()

                    def fv(e, psv=psv, t0=t0, tt=tt):
                        ins = None
                        for k in range(8):
                            ins = e.matmul(psv[:, 0:512], lhsT=H[:, k, t0 + tt * 128:t0 + tt * 128 + 128], rhs=WI[:, k, :], start=(k == 0), stop=(k == 7))
                        return ins
                    S.op("pe", fv, reads=[B("S0")] + BH(b), writes=[psvb], cost=8 * 283)
                    S.op("act", lambda e, psv=psv, gt=gt: e.copy(out=Vall[:, gt, :], in_=psv[:, 0:512]), reads=[psvb], writes=[B("Vall", gt)])

        def hgrn_heads(blocks, W, last_sb, hds, has_s):
            for hd in hds:
                (wq, wqb), (wf, wfb), (wg, wgb) = W.get_n(3)
                if has_s:
                    S.dma("sp", S0[:], sth[:, hd].rearrange("n d e -> d n e"), writes=[B("S0")])
                    S.op("act", lambda e: e.copy(out=S0b[:], in_=S0[:]), reads=[B("S0")], writes=[B("S0b")], n=2048)
                for (b, t0, n, kind) in blocks:
                    ch = 64 if kind == "p" else 8
                    scanm = scanm_p if kind == "p" else scanm_s
                    cmask = cmask_p if kind == "p" else cmask_s
                    psq, psqb = PSg.next(); mm_fm(psq, psqb, wq, wqb, H, BH(b), t0, n, 8)
                    psf, psfb = PSg.next(); mm_fm(psf, psfb, wf, wfb, H, BH(b), t0, n, 8)
                    psg, psgb = PSg.next(); mm_fm(psg, psgb, wg, wgb, H, BH(b), t0, n, 8)
                    BLB = B("LB")
                    nchk = n // ch
                    fs, fsb = Fr.next()
                    S.op("act", lambda e, fs=fs, psg=psg, n=n: e.activation(out=fs[:, 0:n], in_=psg[:, 0:n], func=AF.Exp, scale=-1.0), reads=[psgb], writes=[fsb])
                    S.op("act", lambda e, fs=fs, n=n: e.activation(out=fs[:, 0:n], in_=fs[:, 0:n], func=AF.Ln, bias=1.0), reads=[fsb], writes=[fsb])
                    S.op("act", lambda e, fs=fs, n=n: e.activation(out=fs[:, 0:n], in_=fs[:, 0:n], func=AF.Exp, scale=-1.0), reads=[fsb], writes=[fsb])
                    SGb, SGbb = Bh.next()
                    S.op("dve", lambda e, SGb=SGb, fs=fs, psg=psg, n=n: e.tensor_tensor(out=SGb[:, 0:n], in0=psg[:, 0:n], in1=fs[:, 0:n], op=ALU.mult), reads=[psgb, fsb], writes=[SGbb])
                    fq, fqb = Fr.next()
                    S.op("act", lambda e, fq=fq, psq=psq, n=n: e.copy(out=fq[:, 0:n], in_=psq[:, 0:n]), reads=[psqb], writes=[fqb])
                    f0, f0b = Fr.next(); f1, f1b = Fr.next(); f2, f2b = Fr.next(); f3, f3b = Fr.next()
                    S.op("act", lambda e, f0=f0, psf=psf, n=n: e.activation(out=f0[:, 0:n], in_=psf[:, 0:n], func=AF.Exp, scale=-1.0), reads=[psfb], writes=[f0b])
                    S.op("act", lambda e, f0=f0, n=n: e.activation(out=f0[:, 0:n], in_=f0[:, 0:n], func=AF.Ln, bias=1.0), reads=[f0b], writes=[f0b])
                    S.op("act", lambda e, f0=f0, n=n: e.activation(out=f0[:, 0:n], in_=f0[:, 0:n], func=AF.Exp, scale=-1.0), reads=[f0b], writes=[f0b])
                    S.op("act", lambda e, f0=f0, f1=f1, n=n, hd=hd: e.activation(
                        out=f1[:, 0:n], in_=f0[:, 0:n], func=AF.Identity, bias=LB[:, 4 + hd:5 + hd], scale=LB[:, 8 + hd:9 + hd]),
                        reads=[f0b, BLB], writes=[f1b])
                    S.op("act", lambda e, f0=f0, n=n, hd=hd: e.activation(
                        out=f0[:, 0:n], in_=f0[:, 0:n], func=AF.Ln, bias=LB[:, hd:hd + 1], scale=LB[:, 4 + hd:5 + hd]), reads=[f0b, BLB], writes=[f0b])
                    S.op("dve", lambda e, f0=f0, f2=f2, n=n, scanm=scanm: e.tensor_tensor_scan(
                        out=f2[:, 0:n], data0=scanm[:, 0:n], data1=f0[:, 0:n], initial=0.0, op0=ALU.mult, op1=ALU.add), reads=[f0b, BC], writes=[f2b], n=2 * n)
                    S.op("act", lambda e, f0=f0, f2=f2, n=n: e.activation(out=f0[:, 0:n], in_=f2[:, 0:n], func=AF.Exp), reads=[f2b], writes=[f0b])
                    S.op("act", lambda e, f3=f3, f2=f2, n=n: e.activation(out=f3[:, 0:n], in_=f2[:, 0:n], func=AF.Exp, scale=-1.0), reads=[f2b], writes=[f3b])
                    egl_t, eglb = EGr.next()
                    eg3 = f0[:, 0:n].rearrange("p (c t) -> p c t", t=ch)
                    S.op("act", lambda e, egl_t=egl_t, eg3=eg3, nchk=nchk, ch=ch: e.copy(out=egl_t[:, 0:nchk].unsqueeze(2), in_=eg3[:, :, ch - 1:ch]),
                         reads=[f0b], writes=[eglb], n=16)
                    QT, QTb = Bh.next()
                    S.op("dve", lambda e, QT=QT, fq=fq, f0=f0, n=n: e.tensor_tensor(out=QT[:, 0:n], in0=fq[:, 0:n], in1=f0[:, 0:n], op=ALU.mult),
                         reads=[fqb, f0b], writes=[QTb])
                    KTb, KTbb = Bh.next()
                    S.op("dve", lambda e, KTb=KTb, f1=f1, f3=f3, n=n: e.tensor_tensor(out=KTb[:, 0:n], in0=f1[:, 0:n], in1=f3[:, 0:n], op=ALU.mult),
                         reads=[f1b, f3b], writes=[KTbb])
                    S.op("pool", lambda e, f1=f1, f3=f3, n=n: e.tensor_tensor(out=f1[:, 0:n], in0=f1[:, 0:n], in1=f3[:, 0:n], op=ALU.mult),
                         reads=[f1b, f3b], writes=[f1b])
                    S.op("dve", lambda e, f3=f3, f1=f1, egl_t=egl_t, n=n, ch=ch, nchk=nchk: e.tensor_tensor(
                        out=f3[:, 0:n].rearrange("p (c t) -> p c t", t=ch), in0=f1[:, 0:n].rearrange("p (c t) -> p c t", t=ch),
                        in1=egl_t[:, 0:nchk].unsqueeze(2).broadcast_to([128, nchk, ch]), op=ALU.mult), reads=[f1b, eglb], writes=[f3b])
                    ntile = n // 128
                    gt0 = t0 // 128
                    BV = [B("Vall", gt0 + tt) for tt in range(ntile)]

                    def Vt(tt, hd=hd, gt0=gt0):
                        return Vall[:, gt0 + tt, hd * 128:(hd + 1) * 128]
                    pskh, pskhb = PSg.next()

                    def ftr(e, pskh=pskh, f3=f3, ntile=ntile):
                        ins = None
                        for tt in range(ntile):
                            ins = e.transpose(out=pskh[:, tt * 128:(tt + 1) * 128], in_=f3[:, tt * 128:(tt + 1) * 128], identity=ident[:])
                        return ins
                    S.op("pe", ftr, reads=[f3b, BID], writes=[pskhb], cost=130 * ntile)
                    KHt, KHtb = Bh.next()
                    S.op("act", lambda e, KHt=KHt, pskh=pskh, n=n: e.copy(out=KHt[:, 0:n], in_=pskh[:, 0:n]), reads=[pskhb], writes=[KHtb], n=n)
                    psa, psab = PSg.next()

                    def fat(e, psa=psa, KTb=KTb, QT=QT, ntile=ntile):
                        ins = None
                        for tt in range(ntile):
                            sl = slice(tt * 128, (tt + 1) * 128)
                            ins = e.matmul(psa[:, sl], lhsT=KTb[:, sl], rhs=QT[:, sl], start=True, stop=True)
                        return ins
                    S.op("pe", fat, reads=[KTbb, QTb], writes=[psab], cost=110 * ntile)
                    ATm, ATmb = Bh.next()
                    S.op("dve", lambda e, ATm=ATm, psa=psa, cmask=cmask, n=n, ntile=ntile: e.tensor_tensor(
                        out=ATm[:, 0:n].rearrange("p (a t) -> p a t", t=128), in0=psa[:, 0:n].rearrange("p (a t) -> p a t", t=128),
                        in1=cmask[:].unsqueeze(1).broadcast_to([128, ntile, 128]), op=ALU.mult), reads=[psab, BC], writes=[ATmb], n=n)
                    pso, psob = PSO.next()

                    def fin(e, pso=pso, ATm=ATm, ntile=ntile, Vt=Vt):
                        ins = None
                        for tt in range(ntile):
                            sl = slice(tt * 128, (tt + 1) * 128)
                            ins = e.matmul(pso[:, sl], lhsT=Vt(tt), rhs=ATm[:, sl], start=(tt == 0), stop=False, skip_group_check=True)
                        return ins
                    S.op("pe", fin, reads=[ATmb] + BV, writes=[psob], cost=110 * ntile)
                    if kind == "p":
                        steps = [(tt, cki) for tt in range(ntile) for cki in range(2)]
                        pbank = {}
                        for cki_g in range(2):
                            grp = [(tt, cki_g) for tt in range(ntile)]
                            pss, pssb = PSC.next()

                            def fpp(e, pss=pss, grp=grp, KHt=KHt, Vt=Vt):
                                ins = None
                                for q, (tt, cki) in enumerate(grp):
                                    lo = cki * 64
                                    ins = e.matmul(pss[:, q * 128:(q + 1) * 128], lhsT=KHt[lo:lo + 64, tt * 128:(tt + 1) * 128], rhs=Vt(tt)[lo:lo + 64, :], start=True, stop=True)
                                return ins
                            S.op("pe", fpp, reads=[KHtb] + BV, writes=[pssb], cost=120 * len(grp))
                            for q, st_ in enumerate(grp):
                                pbank[st_] = (pss[:, q * 128:(q + 1) * 128], pssb)
                        cur, curb = Sb16[:, hd, :], B("Sb16", hd)
                        for ci, (tt, cki) in enumerate(steps):
                            c0 = tt * 128 + cki * 64
                            S.op("pe", lambda e, pso=pso, cur=cur, QT=QT, c0=c0: e.matmul(
                                pso[:, c0:c0 + 64], lhsT=cur, rhs=QT[:, c0:c0 + 64], start=False, stop=True, skip_group_check=True),
                                reads=[curb, QTb], writes=[psob], cost=110)
                            pc, pcb = pbank[(tt, cki)]
                            if ci == len(steps) - 1:
                                nxt, nxtb = Sb16[:, hd, :], B("Sb16", hd)
                            else:
                                nxt, nxtb = SbR.next()
                                nxt = nxt[:]
                            S.op("dve", lambda e, nxt=nxt, pc=pc, egl_t=egl_t, ci=ci, hd=hd: e.scalar_tensor_tensor(
                                out=nxt, in0=S32[:, hd, :], scalar=egl_t[:, ci:ci + 1], in1=pc, op0=ALU.mult, op1=ALU.add),
                                reads=[B("S32", hd), eglb, pcb], writes=[nxtb], n=128)
                            S.op("dve", lambda e, pc=pc, egl_t=egl_t, ci=ci, hd=hd: e.scalar_tensor_tensor(
                                out=S32[:, hd, :], in0=S32[:, hd, :], scalar=egl_t[:, ci:ci + 1], in1=pc, op0=ALU.mult, op1=ALU.add),
                                reads=[B("S32", hd), eglb, pcb], writes=[B("S32", hd)], n=128)
                            cur, curb = nxt, nxtb
                    else:
                        def fo(e, pso=pso, QT=QT):
                            ins = None
                            for s_ in range(16):
                                ins = e.matmul(pso[:, 8 * s_:8 * s_ + 8], lhsT=S0b[:, s_, :], rhs=QT[:, 8 * s_:8 * s_ + 8], start=False, stop=True, skip_group_check=True)
                            return ins
                        S.op("pe", fo, reads=[B("S0b"), QTb], writes=[psob], cost=1100)
                        S.op("dve", lambda e, Vt=Vt: e.tensor_tensor(
                            out=Vblk[:], in0=Vt(0).unsqueeze(1).broadcast_to([128, 16, 128]), in1=seqm[:].unsqueeze(2).broadcast_to([128, 16, 128]), op=ALU.mult),
                            reads=[BV[0], BC], writes=[B("Vblk")], n=2048)
                        egl = egl_t[:, 0:16].unsqueeze(2)
                        for j4 in range(4):
                            pss, pssb = PSC.next()
                            S.op("pe", lambda e, pss=pss, KHt=KHt, j4=j4: e.matmul(
                                pss[:, 0:512], lhsT=KHt[:, 0:128], rhs=Vblk[:, 4 * j4:4 * j4 + 4, :].rearrange("p s e -> p (s e)"), start=True, stop=True),
                                reads=[KHtb, B("Vblk")], writes=[pssb], cost=283)
                            S.op("pool", lambda e, j4=j4, egl=egl: e.tensor_tensor(
                                out=S0[:, 4 * j4:4 * j4 + 4, :], in0=S0[:, 4 * j4:4 * j4 + 4, :], in1=egl[:, 4 * j4:4 * j4 + 4, :].broadcast_to([128, 4, 128]), op=ALU.mult),
                                reads=[B("S0"), eglb], writes=[B("S0")])
                            S.op("dve", lambda e, pss=pss, j4=j4: e.tensor_tensor(
                                out=S0[:, 4 * j4:4 * j4 + 4, :], in0=S0[:, 4 * j4:4 * j4 + 4, :], in1=pss[:, 0:512].rearrange("p (s e) -> p s e", e=128), op=ALU.add),
                                reads=[B("S0"), pssb], writes=[B("S0")])
                        S.dma("sp", hg_s[:, hd].rearrange("n d e -> d n e"), S0[:], reads=[B("S0")], nbytes=2 ** 20)
                    f4, f4b = Fr.next()
                    S.op("act", lambda e, f4=f4, pso=pso, n=n: e.copy(out=f4[:, 0:n], in_=pso[:, 0:n]), reads=[psob], writes=[f4b], n=n)
                    OSQ, OSQb = Bh.next()
                    S.op("act", lambda e, OSQ=OSQ, pso=pso, n=n: e.activation(out=OSQ[:, 0:n], in_=pso[:, 0:n], func=AF.Square), reads=[psob], writes=[OSQb], n=n)
                    psn, psnb = PSg.next()
                    S.op("pe", lambda e, psn=psn, OSQ=OSQ, n=n: e.matmul(psn[:, 0:n], lhsT=onesb[:], rhs=OSQ[:, 0:n], start=True, stop=True), reads=[OSQb, BC], writes=[psnb], cost=0.51 * n + 22)
                    fr, frb = Fr.next()
                    S.op("act", lambda e, fr=fr, psn=psn, n=n: e.activation(out=fr[:, 0:n], in_=psn[:, 0:n], func=AF.Ln, bias=epsT[:, 0:1], scale=1.0 / 128),
                         reads=[psnb, BC], writes=[frb])
                    S.op("act", lambda e, fr=fr, n=n: e.activation(out=fr[:, 0:n], in_=fr[:, 0:n], func=AF.Exp, scale=-0.5), reads=[frb], writes=[frb])
                    S.op("dve", lambda e, f4=f4, fr=fr, n=n, hd=hd: e.scalar_tensor_tensor(
                        out=f4[:, 0:n], in0=f4[:, 0:n], scalar=prow(R_HGN + hd), in1=fr[:, 0:n], op0=ALU.mult, op1=ALU.mult), reads=[f4b, frb, BPT], writes=[f4b])
                    S.op("dve", lambda e, f4=f4, SGb=SGb, n=n, t0=t0, hd=hd: e.tensor_tensor(
                        out=Y[:, 4 + hd, t0:t0 + n], in0=f4[:, 0:n], in1=SGb[:, 0:n], op=ALU.mult), reads=[f4b, SGbb], writes=[B("Y", 4 + hd, b)])
            if last_sb and list(hds)[-1] == 3:
                S.dma("sp", hg_p.rearrange("h d e -> d h e"), S32[:], reads=[B("S32", h) for h in range(4)])

        def w_out_phase(blocks, W):
            for oc in range(8):
                w, wb = W.get()
                for (b, t0, n, kind) in blocks:
                    ps, pb = PS.next()
                    mm_fm(ps, pb, w, wb, Y, [B("Y", c, b) for c in range(8)], t0, n, 8)
                    S.op("dve", lambda e, ps=ps, oc=oc, t0=t0, n=n: e.tensor_tensor(
                        out=X[:, oc, t0:t0 + n], in0=ps[:, 0:n], in1=X[:, oc, t0:t0 + n], op=ALU.add), reads=[pb, B("X", oc, b)], writes=[B("X", oc, b)])

        def ffn(l, blocks, last_sb):
            has_s = any(k == "s" for (_, _, _, k) in blocks)
            norm_to_H(blocks, R_NFFN + 8 * l)
            if has_s:
                rows_to_fm(stf[l], 32, NJ, lambda j: STFT[:, j, :], lambda j: [B("STFT", j)])
            for half in range(2):
                specs = []
                for jj in range(11):
                    j = half * 11 + jj
                    specs += [wsrc(w_gu[l], j * 128), wsrc(w_gu[l], DFF + j * 128)]
                for oc in range(8):
                    specs.append((w_dn[l, half * 1408:(half + 1) * 1408, oc * 128:(oc + 1) * 128].rearrange("(k p) n -> p k n", p=128), 11, 128))
                W = WStream(specs)
                for jj in range(11):
                    j = half * 11 + jj
                    (wg, wgb), (wu, wub) = W.get_n(2)
                    for (b, t0, n, kind) in blocks:
                        psg, psgb = PS.next(); mm_fm(psg, psgb, wg, wgb, H, BH(b), t0, n, 8)
                        psu, psub = PS.next(); mm_fm(psu, psub, wu, wub, H, BH(b), t0, n, 8)
                        GX, GXb = GXr.next()
                        data, taps, halo_v, tail_v = ext_views(GX, n, kind, 2)
                        if kind == "p":
                            hsrc, hb = GH[:, l, j, :], B("GH", l, j)
                            tdst, tb = GH[:, l, j, :], B("GH", l, j)
                        else:
                            hsrc, hb = STFT[:, j, :].rearrange("p (s j) -> p s j", j=2), B("STFT", j)
                            tdst, tb = FSS[:, j, :].rearrange("p (s j) -> p s j", j=2), B("STFT", j)
                        S.op("pool", lambda e, halo_v=halo_v, hsrc=hsrc: e.tensor_copy(out=halo_v, in_=hsrc), reads=[hb], writes=[GXb], n=16)
                        S.op("act", lambda e, data=data, psg=psg, n=n, kind=kind: e.copy(out=data, in_=v3(psg[:, 0:n], kind)), reads=[psgb], writes=[GXb])
                        S.op("pool", lambda e, tail_v=tail_v, tdst=tdst: e.tensor_copy(out=tdst, in_=tail_v), reads=[GXb], writes=[tb], n=16)
                        A0, A0b = Fr.next()
                        a0 = v3(A0[:, 0:n], kind)
                        S.op("act", lambda e, a0=a0, psg=psg, n=n, kind=kind, j=j: e.activation(
                            out=a0, in_=v3(psg[:, 0:n], kind), func=AF.Identity, bias=prow(R_FCB + l * NJ + j), scale=prow(R_FCW + (l * 3 + 2) * NJ + j)),
                            reads=[psgb, BPT], writes=[A0b])
                        for k in (1, 0):
                            S.op("dve", lambda e, a0=a0, taps=taps, j=j, k=k: e.scalar_tensor_tensor(
                                out=a0, in0=taps[k], scalar=prow(R_FCW + (l * 3 + k) * NJ + j), in1=a0, op0=ALU.mult, op1=ALU.add),
                                reads=[GXb, A0b, BPT], writes=[A0b])
                        S.op("act", lambda e, A0=A0, n=n: e.activation(out=A0[:, 0:n], in_=A0[:, 0:n], func=AF.Silu), reads=[A0b], writes=[A0b])
                        S.op("dve", lambda e, A0=A0, psu=psu, jj=jj, t0=t0, n=n: e.tensor_tensor(
                            out=Y[:, jj, t0:t0 + n], in0=A0[:, 0:n], in1=psu[:, 0:n], op=ALU.mult), reads=[A0b, psub], writes=[B("Y", jj, b)])
                if half == 0:
                    groups = [[oc] for oc in range(8)]
                elif l == 0:
                    groups = [list(range(8))]
                else:
                    groups = [[0, 1, 2, 3], [4, 5, 6, 7]]
                order = ((g_, oc, blk) for g_ in groups for blk in blocks for oc in g_)
                cur_g, wts_g = None, None
                for g_, oc, (b, t0, n, kind) in order:
                    if g_ is not cur_g:
                        cur_g, wts_g = g_, dict(zip(g_, W.get_n(len(g_))))
                    w, wb = wts_g[oc]
                    ps, pb = PS.next()
                    mm_fm(ps, pb, w, wb, Y, [B("Y", c, b) for c in range(11)], t0, n, 11)
                    S.op("dve", lambda e, ps=ps, oc=oc, t0=t0, n=n: e.tensor_tensor(
                        out=X[:, oc, t0:t0 + n], in0=ps[:, 0:n], in1=X[:, oc, t0:t0 + n], op=ALU.add), reads=[pb, B("X", oc, b)], writes=[B("X", oc, b)])
            if last_sb:
                fm_to_rows(lambda j: GH[:, l, j, :], lambda j: [B("GH", l, j)], NJ, 2, ffn_p[l])
                fm_to_rows(lambda j: FSS[:, j, :], lambda j: [B("STFT", j)], NJ, 32, ffn_s[l])

        def pool_layer(blocks, first_sb, last_sb):
            for (b, t0, n, kind) in blocks:
                R, Rb = norm_R(b, t0, n)
                for g in range(4):
                    w = POOL_W[g]
                    Ds = []
                    for cc in range(2):
                        c = 2 * g + cc
                        HF, HFb = Fr.next()
                        data, _, halo_v, tail_v = ext_views(HF, n, kind, 15)
                        if kind == "p":
                            hsrc, hb = PH[:, c, :], B("PH", c)
                            tdst, tb = PH[:, c, :], B("PH", c)
                        else:
                            hsrc, hb = STPT[:, c, :].rearrange("p (s j) -> p s j", j=15), B("STPT", c)
                            tdst, tb = PSS[:, c, :].rearrange("p (s j) -> p s j", j=15), B("STPT", c)
                        S.op("pool", lambda e, halo_v=halo_v, hsrc=hsrc: e.tensor_copy(out=halo_v, in_=hsrc), reads=[hb], writes=[HFb], n=32)
                        S.op("dve", lambda e, data=data, c=c, R=R, t0=t0, n=n, kind=kind: e.scalar_tensor_tensor(
                            out=data, in0=v3(X[:, c, t0:t0 + n], kind), scalar=prow(R_NMIX + 8 + c), in1=v3(R[:, 0:n], kind), op0=ALU.mult, op1=ALU.mult),
                            reads=[B("X", c, b), Rb, BPT], writes=[HFb])
                        S.op("pool", lambda e, tail_v=tail_v, tdst=tdst: e.tensor_copy(out=tdst, in_=tail_v), reads=[HFb], writes=[tb], n=32)
                        if kind == "p":
                            L = 15 + n

                            def view(buf, lo, hi):
                                return buf[:, lo:hi]
                        else:
                            L = 23

                            def view(buf, lo, hi):
                                return buf[:, 0:16 * 23].rearrange("p (s l) -> p s l", l=23)[:, :, lo:hi]
                        fold = (kind == "p") and not (first_sb and b == 0)
                        cur, curb = HF, HFb
                        for step in range(g if fold else g + 1):
                            sh = 1 << step
                            lo = (1 << (step + 1)) - 1
                            nxt, nxtb = Fr.next()
                            eng = "dve"
                            S.op(eng, lambda e, nxt=nxt, cur=cur, lo=lo, sh=sh, L=L, view=view: e.tensor_tensor(
                                out=view(nxt, lo, L), in0=view(cur, lo, L), in1=view(cur, lo - sh, L - sh), op=ALU.add), reads=[curb], writes=[nxtb])
                            cur, curb = nxt, nxtb
                        if fold:
                            hw = w // 2
                            Tb, Tbb = Bh.next()
                            S.op("act", lambda e, Tb=Tb, cur=cur, n=n: e.copy(out=Tb[:, 0:n], in_=cur[:, 15:15 + n]), reads=[curb], writes=[Tbb], n=n)
                            Th, Thb = ThR.next()
                            S.op("act", lambda e, Th=Th, cur=cur, hw=hw: e.copy(out=Th[:, 0:hw], in_=cur[:, 15 - hw:15]), reads=[curb], writes=[Thb], n=8)
                            Hn, Hnb = Bh.next()
                            S.op("act", lambda e, Hn=Hn, HF=HF, n=n, w=w: e.activation(out=Hn[:, 0:n], in_=HF[:, 15:15 + n], func=AF.Identity, scale=-float(w)),
                                 reads=[HFb], writes=[Hnb], n=n)
                            Ds.append((Tb, Tbb, Th, Thb, Hn, Hnb))
                            continue
                        Dt, Dtb = Bh.next()
                        S.op("dve", lambda e, Dt=Dt, cur=cur, HF=HF, n=n, L=L, w=w, view=view, kind=kind: e.scalar_tensor_tensor(
                            out=v3(Dt[:, 0:n], kind), in0=view(HF, 15, L), scalar=-float(w), in1=view(cur, 15, L), op0=ALU.mult, op1=ALU.add),
                            reads=[curb, HFb], writes=[Dtb])
                        if first_sb and b == 0:
                            tmp, tmpb = Fr.next()

                            S.op("dve", lambda e, tmp=tmp, cur=cur, g=g: e.tensor_tensor(out=tmp[:, 0:15], in0=cur[:, 15:30], in1=rcnt[:, g, 0:15], op=ALU.mult),
                                 reads=[curb, BC], writes=[tmpb])
                            S.op("dve", lambda e, tmp=tmp, Dt=Dt, HF=HF, w=w: e.scalar_tensor_tensor(
                                out=Dt[:, 0:15], in0=HF[:, 15:30], scalar=-float(w), in1=tmp[:, 0:15], op0=ALU.mult, op1=ALU.add),
                                reads=[tmpb, HFb, Dtb], writes=[Dtb])
                        Ds.append((Dt, Dtb))
                    for ec in range(2):
                        ps, pb = PS.next()
                        if len(Ds[0]) == 2:
                            def fp(e, ps=ps, Ds=Ds, g=g, ec=ec, n=n):
                                e.matmul(ps[:, 0:n], lhsT=WP[:, g, 0, ec * 128:(ec + 1) * 128], rhs=Ds[0][0][:, 0:n], start=True, stop=False)
                                return e.matmul(ps[:, 0:n], lhsT=WP[:, g, 1, ec * 128:(ec + 1) * 128], rhs=Ds[1][0][:, 0:n], start=False, stop=True)
                            S.op("pe", fp, reads=[Ds[0][1], Ds[1][1], B("WP")], writes=[pb], cost=2 * (0.51 * n + 22))
                        else:
                            hw = w // 2

                            def fp(e, ps=ps, Ds=Ds, g=g, ec=ec, n=n, hw=hw):
                                ins = None
                                for cc in range(2):
                                    Tb, _, Th, _, Hn, _ = Ds[cc]
                                    wt = WP[:, g, cc, ec * 128:(ec + 1) * 128]
                                    e.matmul(ps[:, 0:n], lhsT=wt, rhs=Tb[:, 0:n], start=(cc == 0), stop=False)
                                    e.matmul(ps[:, hw:n], lhsT=wt, rhs=Tb[:, 0:n - hw], start=False, stop=False, skip_group_check=True)
                                    e.matmul(ps[:, 0:hw], lhsT=wt, rhs=Th[:, 0:hw], start=False, stop=False, skip_group_check=True)
                                    ins = e.matmul(ps[:, 0:n], lhsT=wt, rhs=Hn[:, 0:n], start=False, stop=(cc == 1))
                                return ins
                            rd = [B("WP")]
                            for cc in range(2):
                                rd += [Ds[cc][1], Ds[cc][3], Ds[cc][5]]
                            S.op("pe", fp, reads=rd, writes=[pb], cost=2 * (3 * (0.51 * n + 22) + 70))
                        c = 2 * g + ec
                        S.op("dve", lambda e, ps=ps, c=c, t0=t0, n=n: e.scalar_tensor_tensor(
                            out=X[:, c, t0:t0 + n], in0=ps[:, 0:n], scalar=prow(R_PSC + c), in1=X[:, c, t0:t0 + n], op0=ALU.mult, op1=ALU.add),
                            reads=[pb, B("X", c, b), BPT], writes=[B("X", c, b)])
            if last_sb:
                fm_to_rows(lambda c: PH[:, c, :], lambda c: [B("PH", c)], 8, 15, pool_p)
                for h in range(2):
                    fm_to_rows(lambda c, h=h: PSS[:, c, h * 120:(h + 1) * 120], lambda c: [B("STPT", c)], 8, 120, pool_s[h * 120:(h + 1) * 120, :])

        def final(blocks, dst_of_tile):
            for (b, t0, n, kind) in blocks:
                R, Rb = norm_R(b, t0, n)
                OF = []
                for c in range(8):
                    o, ob = Fr.next()
                    S.op("dve", lambda e, o=o, c=c, R=R, t0=t0, n=n: e.scalar_tensor_tensor(
                        out=o[:, 0:n], in0=X[:, c, t0:t0 + n], scalar=prow(R_NFIN + c), in1=R[:, 0:n], op0=ALU.mult, op1=ALU.mult),
                        reads=[B("X", c, b), Rb, BPT], writes=[ob])
                    OF.append((o, ob))
                for tt in range(n // 128):
                    fm_to_rows(lambda c, tt=tt, OF=OF: OF[c][0][:, tt * 128:(tt + 1) * 128], lambda c, OF=OF: [OF[c][1]], 8, 128, dst_of_tile(b, tt))

        setup()
        for sbi in range(2):
            first_sb, last_sb = (sbi == 0), (sbi == 1)
            blocks = [(0, 0, 512, "p"), (1, 512, 512, "p")]
            tiles = [(xp[sbi * 1024 + tt * 128: sbi * 1024 + (tt + 1) * 128, :], tt * 128, tt // 4) for tt in range(8)]
            if last_sb:
                blocks.append((2, 1024, 128, "s"))
                tiles.append((xs, 1024, 2))
            load_x(tiles)
            if first_sb:
                late_setup()
            norm_to_H(blocks, R_NMIX)
            specs = []
            for c in range(4):
                hd = c
                specs += [wsrc(w_in, C_BQ + hd * 128), wsrc(w_in, C_BF + hd * 128), wsrc(w_in, C_BG + hd * 128)]
                specs += [wsrc(w_in, C_AC + c * 128), wsrc(w_in, C_AV + c * 128), wsrc(w_in, C_AB + c * 128)]
            for oc in range(8):
                specs.append(wsrc(w_out, oc * 128))
            W = WStream(specs)
            hgrn_prologue(blocks)
            has_s = any(k == "s" for (_, _, _, k) in blocks)
            for i in range(4):
                hgrn_heads(blocks, W, last_sb, [i], has_s)
                mixer_a(blocks, W, [i])
            if last_sb:
                fm_to_rows(lambda c: UH[:, c, :], lambda c: [B("UH", c)], 4, 2, conv_p)
                fm_to_rows(lambda c: CSS[:, c, :], lambda c: [B("STCT", c)], 4, 32, conv_s)
            if first_sb:
                for g in range(4):
                    S.dma("pool", WP[:, g, :, :], w_pool[g].rearrange("(k p) n -> p k n", p=128), writes=[B("WP")])
                for g in range(4):
                    S.op("pool", lambda e, g=g: e.tensor_scalar(out=WP[:, g, :, :], in0=WP[:, g, :, :], scalar1=1.0 / POOL_W[g], scalar2=0.0,
                                                                op0=ALU.mult, op1=ALU.add), reads=[B("WP")], writes=[B("WP")], n=512)
            w_out_phase(blocks, W)
            ffn(0, blocks, last_sb)
            pool_layer(blocks, first_sb, last_sb)
            ffn(1, blocks, last_sb)

            def dst_of_tile(b, tt, sbi=sbi):
                if b == 2:
                    return y_s
                r0 = sbi * 1024 + b * 512 + tt * 128
                return y_p[r0:r0 + 128, :]
            final(blocks, dst_of_tile)
        S.finish()
        S.emit(nc, esem, dsem)
    return nc, S


_CACHE = {}


def kernel(x_prompt, x_sample, state_conv_a, state_hgrn, state_pool, state_ffn, norm_mix, norm_ffn,
           norm_final, w_in, conv_a_w, hgrn_lower_bounds, hgrn_norm, w_out, pool_w, pool_scale,
           ffn_w_gu, ffn_conv_w, ffn_conv_b, ffn_w_down):
    f = lambda a: np.ascontiguousarray(np.asarray(a, dtype=np.float32))
    ncores = 8
    rows = [f(norm_mix).reshape(16, 128), f(norm_ffn).reshape(16, 128), f(norm_final).reshape(8, 128),
            f(conv_a_w).reshape(12, 128), f(hgrn_lower_bounds).reshape(12, 128), f(hgrn_norm).reshape(4, 128),
            f(pool_scale).reshape(8, 128), f(ffn_conv_w).reshape(2 * 3 * NJ, 128), f(ffn_conv_b).reshape(2 * NJ, 128)]
    prm = np.concatenate(rows + [np.zeros((256 - 252, 128), np.float32)], axis=0)
    assert prm.shape == (256, 128)
    shared = {"prm": prm, "w_in": f(w_in)[0], "w_out": f(w_out)[0], "w_pool": f(pool_w)[0],
              "w_gu": f(ffn_w_gu), "w_dn": f(ffn_w_down)}
    xpf, xsf = f(x_prompt), f(x_sample)
    sc, sh, sp_, sf = f(state_conv_a), f(state_hgrn), f(state_pool), f(state_ffn)
    in_maps = []
    for i in range(ncores):
        sl = slice(16 * i, 16 * i + 16)
        m = dict(shared)
        m["xp"] = xpf[i]
        m["xs"] = np.ascontiguousarray(xsf[sl].reshape(128, D))
        m["stc"] = np.ascontiguousarray(sc[0, sl].reshape(32, 512))
        m["sth"] = np.ascontiguousarray(sh[0, sl])
        m["stp"] = np.ascontiguousarray(sp_[0, sl].reshape(240, D))
        m["stf"] = np.ascontiguousarray(sf[:, sl].reshape(2, 32, DFF))
        in_maps.append(m)
    if "nc" not in _CACHE:
        _CACHE["nc"] = build_nc()[0]
    nc = _CACHE["nc"]
    res = run_bass_kernel_spmd(nc, in_maps, core_ids=list(range(ncores)))
    R = res.results
    if DEBUG:
        _CACHE["dbg"] = np.asarray(R[0]["dbg"])
    cat = lambda k, ax=0: np.concatenate([np.asarray(r[k], dtype=np.float32) for r in R], axis=ax)
    y_prompt = np.stack([np.asarray(r["y_p"], np.float32) for r in R], 0)
    y_sample = cat("y_s").reshape(128, 8, D)
    conv_p = np.stack([np.asarray(r["conv_p"], np.float32) for r in R], 0)[None]
    conv_s = cat("conv_s").reshape(1, 128, 2, 512)
    hg_p = np.stack([np.asarray(r["hg_p"], np.float32) for r in R], 0)[None]
    hg_s = cat("hg_s")[None]
    pool_p = np.stack([np.asarray(r["pool_p"], np.float32) for r in R], 0)[None]
    pool_s = cat("pool_s").reshape(1, 128, 15, D)
    ffn_p = np.stack([np.asarray(r["ffn_p"], np.float32) for r in R], 1)
    ffn_s = cat("ffn_s", 1).reshape(2, 128, 2, DFF)
    return (y_prompt, y_sample, conv_p, conv_s, hg_p, hg_s, pool_p, pool_s, ffn_p, ffn_s)
```
